# Optimizing a Trainium2 kernel written in Bass

```python
import jax, jax.numpy as jnp
from jax import lax
import numpy as np

D_MODEL = 1024
BATCH = 8
SEQ = 2048
DEPTH = 2

CHUNK = 64
RMS_EPS = 1e-6

LRU_HEADS = 8
LRU_WIDTH = D_MODEL // 2
LRU_HEAD_DIM = LRU_WIDTH // LRU_HEADS
LRU_CONV = 4
LRU_C = 8.0
SC_HEADS = 8
SC_WIDTH = D_MODEL // 2
SC_CONV = 3
EVEN_IN = 2 * LRU_WIDTH + 3 * SC_WIDTH

POOL_WINDOWS = (2, 4, 8, 16)
POOL_WIDTH = D_MODEL // 2
POOL_GROUP = POOL_WIDTH // len(POOL_WINDOWS)
SG_HEADS = 8
SG_WIDTH = D_MODEL // 2
SG_HEAD_DIM = SG_WIDTH // SG_HEADS
SG_BLOCK = 128
ODD_IN = POOL_WIDTH + 2 * SG_WIDTH

FFN_HIDDEN = -(-8 * D_MODEL // (3 * 256)) * 256

N_EVEN = (DEPTH + 1) // 2
N_ODD = DEPTH // 2

kernel_name = "hybrid_rglru_shortconv_pool_gmlp_trunk"


def rms_norm(x, g):
    xf = x.astype(jnp.float32)
    y = xf * lax.rsqrt(jnp.mean(xf * xf, axis=-1, keepdims=True) + RMS_EPS)
    return y.astype(x.dtype) * g


def causal_dwconv(x, w):
    k = w.shape[0]
    return lax.conv_general_dilated(
        x, w[:, None, :], window_strides=(1,), padding=((k - 1, 0),),
        dimension_numbers=("NWC", "WIO", "NWC"), feature_group_count=x.shape[-1])


def rg_lru(x, w_r, b_r, w_i, b_i, lam):
    bsz, s, _ = x.shape
    xh = x.reshape(bsz, s, LRU_HEADS, LRU_HEAD_DIM)
    r = jax.nn.sigmoid(jnp.einsum("bshd,hde->bshe", xh, w_r) + b_r).reshape(bsz, s, LRU_WIDTH)
    i = jax.nn.sigmoid(jnp.einsum("bshd,hde->bshe", xh, w_i) + b_i).reshape(bsz, s, LRU_WIDTH)
    log_a = -LRU_C * r.astype(jnp.float32) * jax.nn.softplus(-lam.astype(jnp.float32))
    a = jnp.exp(log_a)
    b = jnp.sqrt(-jnp.expm1(2.0 * log_a)) * (i * x).astype(jnp.float32)

    def combine(left, right):
        a1, b1 = left
        a2, b2 = right
        return a1 * a2, a2 * b1 + b2

    _, h = lax.associative_scan(combine, (a, b), axis=1)
    return h.astype(x.dtype)


def multiscale_pool(x, w, scale):
    bsz, s, _ = x.shape
    xf = x.astype(jnp.float32).reshape(bsz, s, len(POOL_WINDOWS), POOL_GROUP)
    cs = jnp.cumsum(xf, axis=1)
    count = jnp.arange(1, s + 1, dtype=jnp.float32)
    outs = []
    for g, win in enumerate(POOL_WINDOWS):
        c = cs[:, :, g]
        lag = jnp.pad(c, ((0, 0), (win, 0), (0, 0)))[:, :s]
        mean = (c - lag) / jnp.minimum(count, float(win))[None, :, None]
        outs.append(mean - xf[:, :, g])
    p = jnp.stack(outs, axis=2).astype(x.dtype)
    y = jnp.einsum("bsgc,gcd->bsgd", p, w).reshape(bsz, s, POOL_WIDTH)
    return y * scale


def spatial_gate(v, w_s, b_s):
    bsz, s, _ = v.shape
    vb = v.reshape(bsz, s // SG_BLOCK, SG_BLOCK, SG_HEADS, SG_HEAD_DIM)
    ck = np.arange(SG_BLOCK) // CHUNK
    mask = ck[None, :] <= ck[:, None]
    ws = jnp.where(mask[None], w_s, 0)
    g = jnp.einsum("hts,bnshc->bnthc", ws, vb) + b_s.T[:, :, None]
    return g.reshape(bsz, s, SG_WIDTH)


def even_mixer(h, w_in, conv_w, conv_b, w_r, b_r, w_i, b_i, lam, sc_conv_w, w_out):
    z = h @ w_in
    xa, ga, bg, cg, hs = jnp.split(
        z, [LRU_WIDTH, 2 * LRU_WIDTH, 2 * LRU_WIDTH + SC_WIDTH, 2 * LRU_WIDTH + 2 * SC_WIDTH], axis=-1)
    xa = causal_dwconv(xa, conv_w) + conv_b
    ya = rg_lru(xa, w_r, b_r, w_i, b_i, lam) * jax.nn.gelu(ga)
    yb = bg * causal_dwconv(cg * hs, sc_conv_w)
    return jnp.concatenate([ya, yb], axis=-1) @ w_out


def odd_mixer(h, w_in, pool_w, pool_scale, sg_norm, sg_w, sg_b, w_out):
    z = h @ w_in
    xp, u, v = jnp.split(z, [POOL_WIDTH, POOL_WIDTH + SG_WIDTH], axis=-1)
    yc = multiscale_pool(xp, pool_w, pool_scale)
    yd = u * spatial_gate(rms_norm(v, sg_norm), sg_w, sg_b)
    return jnp.concatenate([yc, yd], axis=-1) @ w_out


def swiglu(h, w_gate, w_up, w_down):
    return (jax.nn.silu(h @ w_gate) * (h @ w_up)) @ w_down


def setup_inputs(seed: int = 0) -> dict:
    key = jax.random.key(seed)
    ks = iter(jax.random.split(key, 40))

    def nrm(shape, scale):
        return jax.random.normal(next(ks), shape, jnp.float32) * scale

    def gain(shape):
        return 1.0 + nrm(shape, 0.05)

    x = nrm((BATCH, SEQ, D_MODEL), 1.0)
    e_w_in = nrm((N_EVEN, D_MODEL, EVEN_IN), D_MODEL ** -0.5)
    e_conv_w = nrm((N_EVEN, LRU_CONV, LRU_WIDTH), LRU_CONV ** -0.5)
    e_conv_b = nrm((N_EVEN, LRU_WIDTH), 0.02)
    e_w_r = nrm((N_EVEN, LRU_HEADS, LRU_HEAD_DIM, LRU_HEAD_DIM), LRU_HEAD_DIM ** -0.5)
    e_b_r = nrm((N_EVEN, LRU_HEADS, LRU_HEAD_DIM), 0.02)
    e_w_i = nrm((N_EVEN, LRU_HEADS, LRU_HEAD_DIM, LRU_HEAD_DIM), LRU_HEAD_DIM ** -0.5)
    e_b_i = nrm((N_EVEN, LRU_HEADS, LRU_HEAD_DIM), 0.02)
    u = jax.random.uniform(next(ks), (N_EVEN, LRU_WIDTH), jnp.float32, 0.9, 0.999)
    a0 = u ** (1.0 / LRU_C)
    e_lam = jnp.log(a0) - jnp.log1p(-a0)
    e_sc_conv_w = nrm((N_EVEN, SC_CONV, SC_WIDTH), SC_CONV ** -0.5)
    e_w_out = nrm((N_EVEN, LRU_WIDTH + SC_WIDTH, D_MODEL), (LRU_WIDTH + SC_WIDTH) ** -0.5)

    o_w_in = nrm((N_ODD, D_MODEL, ODD_IN), D_MODEL ** -0.5)
    o_pool_w = nrm((N_ODD, len(POOL_WINDOWS), POOL_GROUP, POOL_GROUP), POOL_GROUP ** -0.5)
    o_pool_scale = gain((N_ODD, POOL_WIDTH))
    o_sg_norm = gain((N_ODD, SG_WIDTH))
    o_sg_w = nrm((N_ODD, SG_HEADS, SG_BLOCK, SG_BLOCK), SG_BLOCK ** -0.5)
    o_sg_b = gain((N_ODD, SG_HEADS, SG_BLOCK))
    o_w_out = nrm((N_ODD, POOL_WIDTH + SG_WIDTH, D_MODEL), (POOL_WIDTH + SG_WIDTH) ** -0.5)

    norm_mix_pre = gain((DEPTH, D_MODEL))
    norm_mix_post = gain((DEPTH, D_MODEL))
    norm_ffn_pre = gain((DEPTH, D_MODEL))
    norm_ffn_post = gain((DEPTH, D_MODEL))
    w_gate = nrm((DEPTH, D_MODEL, FFN_HIDDEN), D_MODEL ** -0.5)
    w_up = nrm((DEPTH, D_MODEL, FFN_HIDDEN), D_MODEL ** -0.5)
    w_down = nrm((DEPTH, FFN_HIDDEN, D_MODEL), FFN_HIDDEN ** -0.5)
    return {
        "x": x,
        "e_w_in": e_w_in, "e_conv_w": e_conv_w, "e_conv_b": e_conv_b,
        "e_w_r": e_w_r, "e_b_r": e_b_r, "e_w_i": e_w_i, "e_b_i": e_b_i,
        "e_lam": e_lam, "e_sc_conv_w": e_sc_conv_w, "e_w_out": e_w_out,
        "o_w_in": o_w_in, "o_pool_w": o_pool_w, "o_pool_scale": o_pool_scale,
        "o_sg_norm": o_sg_norm, "o_sg_w": o_sg_w, "o_sg_b": o_sg_b, "o_w_out": o_w_out,
        "norm_mix_pre": norm_mix_pre, "norm_mix_post": norm_mix_post,
        "norm_ffn_pre": norm_ffn_pre, "norm_ffn_post": norm_ffn_post,
        "w_gate": w_gate, "w_up": w_up, "w_down": w_down,
    }


def reference(x, e_w_in, e_conv_w, e_conv_b, e_w_r, e_b_r, e_w_i, e_b_i, e_lam,
              e_sc_conv_w, e_w_out, o_w_in, o_pool_w, o_pool_scale, o_sg_norm,
              o_sg_w, o_sg_b, o_w_out, norm_mix_pre, norm_mix_post, norm_ffn_pre,
              norm_ffn_post, w_gate, w_up, w_down):
    h = x
    for layer in range(DEPTH):
        hn = rms_norm(h, norm_mix_pre[layer])
        if layer % 2 == 0:
            j = layer // 2
            m = even_mixer(hn, e_w_in[j], e_conv_w[j], e_conv_b[j], e_w_r[j], e_b_r[j],
                           e_w_i[j], e_b_i[j], e_lam[j], e_sc_conv_w[j], e_w_out[j])
        else:
            j = layer // 2
            m = odd_mixer(hn, o_w_in[j], o_pool_w[j], o_pool_scale[j], o_sg_norm[j],
                          o_sg_w[j], o_sg_b[j], o_w_out[j])
        h = h + rms_norm(m, norm_mix_post[layer])
        f = swiglu(rms_norm(h, norm_ffn_pre[layer]), w_gate[layer], w_up[layer], w_down[layer])
        h = h + rms_norm(f, norm_ffn_post[layer])
    return h
```

```python
import numpy as np
from contextlib import ExitStack
import concourse.bass as bass
import concourse.mybir as mybir
from concourse.bass_utils import run_bass_kernel_spmd

F32 = mybir.dt.float32
BF16 = mybir.dt.bfloat16
AF = mybir.ActivationFunctionType
ALU = mybir.AluOpType
GRAN = 128

P = 128
D = 1024
KC = 8
SEQ = 2048
T = 1024
NT = 512
NTILE = T // NT
NPASS = SEQ // T
HID = 2816
HC = HID // P
EPS = 1e-6
LRU_C = 8.0
WINS = (2, 4, 8, 16)
TMPW = 528
LRU_OFFS = (0, 0, 3, 3)
SP_OFFS = (2, 9, 11)
SC_OFFS = (1, 6)
LRU_T1 = 7
SLOT_COLS = 4096
RING_COLS = 5 * SLOT_COLS


def _esize(dt):
    if dt == F32:
        return 4
    if dt == BF16:
        return 2
    return mybir.dt.size(dt)


class Sched:
    ENG = ("pe", "act", "dve", "pool", "sp")

    def __init__(self, nc, stack):
        self.nc = nc
        self.stack = stack
        self.ops = {e: [] for e in self.ENG}
        self.sems = {}
        for e in self.ENG:
            self.sems["es_" + e] = stack.enter_context(nc.semaphore("es_" + e))
        self.ecnt = {e: 0 for e in self.ENG}
        self.semcnt = {}
        self.waited = {e: {} for e in self.ENG}
        self.lastw = {}
        self.readers = {}
        self.base = {}

    def _pad_to_gran(self, end):
        nxt = (end + 31) // 32 * 32
        if nxt % GRAN:
            self.npad = getattr(self, "npad", 0) + 1
            p = self.stack.enter_context(self.nc.sbuf_tensor(f"pad{self.npad}", [128, (GRAN - nxt % GRAN) // 2], BF16))
            assert int(self.nc.lookup_mloc(p).addr) == nxt, (int(self.nc.lookup_mloc(p).addr), nxt)

    def sbuf(self, name, shape, dt):
        if not self.base:
            p0 = self.stack.enter_context(self.nc.sbuf_tensor("pad_first", [128, 16], BF16))
            self._pad_to_gran(int(self.nc.lookup_mloc(p0).addr) + 32)
        t = self.stack.enter_context(self.nc.sbuf_tensor(name, list(shape), dt))
        addr = int(self.nc.lookup_mloc(t).addr)
        assert addr % GRAN == 0, (name, addr)
        self.base[name] = (0, addr)
        size = _esize(dt)
        for d in list(shape)[1:]:
            size *= int(d)
        self._pad_to_gran(addr + size)
        return t

    def psum(self, name, shape, dt):
        t = self.stack.enter_context(self.nc.psum_tensor(name, list(shape), dt))
        ml = self.nc.lookup_mloc(t)
        self.base[name] = (1, int(ml.bank) * 2048 + int(ml.addr))
        return t

    def dma_sem(self, name):
        self.sems[name] = self.stack.enter_context(self.nc.semaphore(name))
        self.semcnt[name] = 0
        return name

    def _grans(self, ap):
        t = ap.tensor
        nm = t.name
        if nm not in self.base:
            return ()
        space, b0 = self.base[nm]
        row = 1
        for s in list(t.shape)[1:]:
            row *= int(s)
        es = _esize(ap.dtype)
        tes = _esize(t.dtype)
        row_ap = row * tes // es
        off = int(ap.offset) % row_ap
        ext = 1
        for (st, cnt) in list(ap.ap)[1:]:
            ext += abs(int(st)) * (int(cnt) - 1)
        lo = b0 + off * es
        hi = b0 + (off + ext) * es
        return [(space, g) for g in range(lo // GRAN, (hi - 1) // GRAN + 1)]

    def op(self, eng, fn, reads=(), writes=(), dma=None):
        deps = {}
        rg = []
        for ap in reads:
            rg.extend(self._grans(ap))
        wg = []
        for ap in writes:
            wg.extend(self._grans(ap))
        own = "es_" + eng
        own_raw = 0
        track_own = dma is None and eng in ("act", "dve", "pool")
        for g in rg:
            lw = self.lastw.get(g)
            if lw is not None:
                if deps.get(lw[0], 0) < lw[1]:
                    deps[lw[0]] = lw[1]
                if track_own and lw[0] == own and lw[1] > own_raw:
                    own_raw = lw[1]
        for g in wg:
            lw = self.lastw.get(g)
            if lw is not None:
                if deps.get(lw[0], 0) < lw[1]:
                    deps[lw[0]] = lw[1]
                if track_own and lw[0] == own and lw[1] > own_raw:
                    own_raw = lw[1]
            rd = self.readers.get(g)
            if rd:
                for k, v in rd.items():
                    if deps.get(k, 0) < v:
                        deps[k] = v
                    if track_own and k == own and v > own_raw:
                        own_raw = v
        waits = []
        wd = self.waited[eng]
        for k, v in deps.items():
            if k == own:
                continue
            if wd.get(k, 0) >= v:
                continue
            wd[k] = v
            waits.append((k, v))
        if own_raw and wd.get(own, 0) < own_raw:
            wd[own] = own_raw
            waits.append((own, own_raw))
        if dma is None:
            self.ecnt[eng] += 1
            tok = (own, self.ecnt[eng])
            inc = (own, 1)
        else:
            self.semcnt[dma] += 16
            tok = (dma, self.semcnt[dma])
            inc = (dma, 16)
        for g in wg:
            self.lastw[g] = tok
            self.readers[g] = {}
        for g in rg:
            d = self.readers.setdefault(g, {})
            if d.get(tok[0], 0) < tok[1]:
                d[tok[0]] = tok[1]
        self.ops[eng].append((waits, fn, inc))
        return tok

    def wait_all(self, eng, toks):
        self.ops[eng].append((list(toks), None, None))

    def emit(self):
        nc = self.nc
        with nc.Block() as block:
            def mk(ename):
                def body(e):
                    for waits, fn, inc in self.ops[ename]:
                        for k, v in waits:
                            e.wait_ge(self.sems[k], v)
                        if fn is None:
                            continue
                        ins = fn(e)
                        ins.then_inc(self.sems[inc[0]], inc[1])
                return body
            block.tensor(mk("pe"))
            block.scalar(mk("act"))
            block.vector(mk("dve"))
            block.gpsimd(mk("pool"))
            block.sync(mk("sp"))


class FreeList:
    def __init__(self, items):
        self.items = list(items)
        self.free_ = list(items)

    def alloc(self):
        assert self.free_, "out of temporaries"
        return self.free_.pop(0)

    def free(self, it):
        self.free_.append(it)


class Rot:
    def __init__(self, items):
        self.items = items
        self.i = 0

    def get(self):
        x = self.items[self.i % len(self.items)]
        self.i += 1
        return x


class PsumAlloc:
    def __init__(self, S):
        self.b = [S.psum(f"psb{i}", [P, NT], F32) for i in range(8)]
        self.i = 0
        self.held = set()

    def get(self):
        while True:
            k = self.i % 8
            self.i += 1
            if k not in self.held:
                return self.b[k]

    def alloc(self):
        k = self.hold(1)[0]
        return k, self.b[k]

    def free(self, k):
        self.held.discard(k)

    def hold(self, n):
        out = []
        assert len(self.held) + n <= 8, "out of PSUM banks"
        while len(out) < n:
            k = self.i % 8
            self.i += 1
            if k not in self.held:
                self.held.add(k)
                out.append(k)
        return out

    def release(self, ks):
        for k in ks:
            self.held.discard(k)


class WRing:
    def __init__(self, S, ncols, plan, nsem=8):
        self.S = S
        self.ncols = ncols
        self.buf = S.sbuf("wring", [P, ncols], BF16)
        self.sems = [S.dma_sem(f"wrsem{i}") for i in range(nsem)]
        self.plan = plan
        self.pos = []
        self.nslot = ncols // SLOT_COLS
        for i, (tag, n, _) in enumerate(plan):
            assert n <= SLOT_COLS
            self.pos.append(((i % self.nslot) * SLOT_COLS, n))
        self.issued = 0
        self.cur = 0

    def view(self, i):
        off, n = self.pos[i]
        return self.buf[:, off:off + n]

    def _overlaps(self, a, b):
        return a[0] < b[0] + b[1] and b[0] < a[0] + a[1]

    def prime(self, oldest_live):
        while self.issued < len(self.plan):
            u = self.issued
            if u - oldest_live >= len(self.sems):
                break
            if u - self.nslot >= oldest_live:
                break
            v_ = self.view(u)
            sem = self.sems[u % len(self.sems)]
            for dst, src in self.plan[u][2](v_):
                self.S.op("pool", lambda e, dst=dst, src=src: e.dma_start(out=dst, in_=src), writes=[dst], dma=sem)
            self.issued += 1

    def next(self, tag, live=0):
        i = self.cur
        assert self.plan[i][0] == tag, (self.plan[i][0], tag)
        self.cur += 1
        self.prime(i - live)
        assert self.issued > i, ("ring too small for", tag)
        return self.view(i)


def VC_NORM(l, which, c):
    return l * 32 + which * 8 + c
VC_CONVW = 64
VC_CONVB = 80
VC_BR = 84
VC_BI = 88
VC_LAM = 92
VC_SCW = 96
VC_PSC = 108
VC_NL = 112
VC_NL2 = 116
VC_EPS = 120
VC_ONE = 121
VC_T0 = 122


PHASE_MARKS = []


def build_program(layers=(0, 1)):
    del PHASE_MARKS[:]
    nc = bass.Bass("TRN2", target_bir_lowering=False)

    def din(name, shape):
        return nc.dram_tensor(name, list(shape), F32, kind="ExternalInput").ap()

    x = din("x", [SEQ, D])
    e_w_in = din("e_w_in", [D, 2560])
    e_w_out = din("e_w_out", [D, D])
    o_w_in = din("o_w_in", [D, 1536])
    o_w_out = din("o_w_out", [D, D])
    w_gate = din("w_gate", [2, D, HID])
    w_up = din("w_up", [2, D, HID])
    w_down = din("w_down", [2, HID, D])
    pool_w = din("pool_w", [P, 4, P])
    wri = din("wri", [P, 8, P])
    sgwT = din("sgwT", [P, 8, P])
    sgb = din("sgb", [P, 4, P])
    sgn = din("sgn", [P, 512])
    vecs = din("vecs", [P, P])
    invc = din("invc", [P, 4, 16])
    out = nc.dram_tensor("out", [SEQ, D], F32, kind="ExternalOutput").ap()

    with ExitStack() as st:
        S = Sched(nc, st)
        op = S.op
        H = S.sbuf("H", [P, KC, T], F32)
        BIG = S.sbuf("BIG", [P, 16384], BF16)
        XNv = BIG[:, 0:8192].rearrange("p (t c n) -> p t c n", t=NTILE, c=KC)
        BIGF = BIG[:].bitcast(F32)
        MFv = BIGF.rearrange("p (t c n) -> p t c n", t=NTILE, c=KC)

        def XNt(c, tile):
            return XNv[:, tile, c, :]

        def MFt(c, tile):
            return MFv[:, tile, c, :]
        A = S.sbuf("A", [P, HC, T], BF16)
        Y = A
        SQR = Rot([S.sbuf(f"sq{i}", [P, NT], BF16) for i in range(4)])
        RSTD = Rot([S.sbuf(f"rstd{i}", [P, NT + 8], F32) for i in range(2)])
        TMP = Rot([S.sbuf(f"tmp{i}", [P, TMPW], F32) for i in range(16)])
        TMPB = Rot([S.sbuf(f"tmpb{i}", [P, NT], BF16) for i in range(4)])
        STAT = S.sbuf("STAT", [P, 8, 64], F32)
        VEC = S.sbuf("VEC", [P, P], F32)
        WRI = S.sbuf("WRI", [P, 8, P], BF16)
        SGW = S.sbuf("SGW", [P, 8, P], BF16)
        POOLW = S.sbuf("POOLW", [P, 4, P], BF16)
        SGB = S.sbuf("SGB", [P, 4, P], F32)
        SGN = S.sbuf("SGN", [P, 512], F32)
        INVC = S.sbuf("INVC", [P, 4, 16], F32)
        IDENT = S.sbuf("IDENT", [P, P], F32)
        ONES = S.sbuf("ONES", [P, P], BF16)
        CARX = S.sbuf("CARX", [P, 4, 4], F32)
        CARP = S.sbuf("CARP", [P, 4, 4], F32)
        CARH = S.sbuf("CARH", [P, 4], F32)
        CARQ = S.sbuf("CARQ", [P, 4, 16], F32)
        PS = PsumAlloc(S)
        NSTG = 4
        xsem = [S.dma_sem(f"xsem{i}") for i in range(NSTG)]
        osem = [S.dma_sem(f"osem{i}") for i in range(NSTG)]

        def vc(col):
            return VEC[:, col:col + 1]

        ewv = e_w_in.rearrange("(kc p) n -> p kc n", p=P)
        owv = o_w_in.rearrange("(kc p) n -> p kc n", p=P)
        wouts = [e_w_out.rearrange("(kc p) n -> p kc n", p=P), o_w_out.rearrange("(kc p) n -> p kc n", p=P)]

        def k3(v, k):
            return v.rearrange("p (k n) -> p k n", k=k)

        def unit(tag, src3):
            k, n = int(src3.shape[1]), int(src3.shape[2])
            return (tag, k * n, lambda v, src3=src3, k=k: [(k3(v, k), src3)])

        plan = []
        for ps_ in range(NPASS):
            for l in layers:
                if l == 0:
                    for nm, g in (("xa", 0), ("ga", 1), ("cg", 3), ("hs", 4), ("bg", 2)):
                        plan.append(unit(("ein", nm), ewv[:, :, g * 512:(g + 1) * 512]))
                else:
                    for nm, g in (("xp", 0), ("v", 2), ("u", 1)):
                        plan.append(unit(("oin", nm), owv[:, :, g * 512:(g + 1) * 512]))
                for g in range(2):
                    plan.append(unit(("wout", g), wouts[l][:, :, g * 512:(g + 1) * 512]))
                gv = w_gate[l].rearrange("(kc p) n -> p kc n", p=P)
                uv = w_up[l].rearrange("(kc p) n -> p kc n", p=P)
                for g in range(6):
                    n0 = g * 512
                    nn = min(512, HID - n0)
                    plan.append(unit(("gate", g), gv[:, :, n0:n0 + nn]))
                    plan.append(unit(("up", g), uv[:, :, n0:n0 + nn]))
                dv = w_down[l].rearrange("(hc p) n -> p hc n", p=P)
                for g in range(4):
                    for kh in range(2):
                        plan.append(unit(("down", g, kh), dv[:, kh * 11:(kh + 1) * 11, g * 256:(g + 1) * 256]))
        WR = WRing(S, RING_COLS, plan)

        ncl = [0]

        def cload(dst, src, q="sp"):
            ncl[0] += 1
            op(q, lambda e: e.dma_start(out=dst, in_=src), writes=[dst], dma=S.dma_sem(f"csem{ncl[0]}"))

        cload(VEC[:], vecs[:])
        cload(SGB[:], sgb[:])
        cload(SGN[:], sgn[:])
        cload(INVC[:], invc[:])
        cload(WRI[:], wri[:], "pool")
        cload(SGW[:], sgwT[:], "pool")
        cload(POOLW[:], pool_w[:], "pool")
        op("dve", lambda e: e.memset(SGW[64:128, :, 0:64], 0.0), reads=[SGW[:]], writes=[SGW[:]])
        op("pool", lambda e: e.memset(IDENT[:], 0.0), writes=[IDENT[:]])
        op("pool", lambda e: e.affine_select(out=IDENT[:], in_=IDENT[:], pattern=[[-1, P]], compare_op=ALU.not_equal,
                                             fill=1.0, base=0, channel_multiplier=1),
           reads=[IDENT[:]], writes=[IDENT[:]])
        op("dve", lambda e: e.memset(ONES[:], 1.0), writes=[ONES[:]])
        op("dve", lambda e: e.memset(vc(VC_EPS), EPS), reads=[VEC[:]], writes=[VEC[:]])
        op("dve", lambda e: e.memset(vc(VC_ONE), 1.0), reads=[VEC[:]], writes=[VEC[:]])
        for tcar in (CARX, CARP, CARH, CARQ):
            op("dve", lambda e, tcar=tcar: e.memset(tcar[:], 0.0), writes=[tcar[:]])
        yv = VEC[:, VC_T0:VC_T0 + 4]
        nl = VEC[:, VC_NL:VC_NL + 4]
        nl2 = VEC[:, VC_NL2:VC_NL2 + 4]
        op("act", lambda e: e.activation(out=yv, in_=VEC[:, VC_LAM:VC_LAM + 4], func=AF.Exp, scale=-1.0),
           reads=[VEC[:]], writes=[VEC[:]])
        op("dve", lambda e: e.tensor_scalar(out=nl, in0=yv, scalar1=1.0 / 5.0, scalar2=-0.25, op0=ALU.mult, op1=ALU.add),
           reads=[VEC[:]], writes=[VEC[:]])
        for cst in (1.0 / 3.0, -0.5, 1.0):
            op("dve", lambda e: e.tensor_tensor(out=nl, in0=nl, in1=yv, op=ALU.mult), reads=[VEC[:]], writes=[VEC[:]])
            op("dve", lambda e, cst=cst: e.tensor_scalar(out=nl, in0=nl, scalar1=cst, scalar2=None, op0=ALU.add),
               reads=[VEC[:]], writes=[VEC[:]])
        op("dve", lambda e: e.tensor_tensor(out=nl, in0=nl, in1=yv, op=ALU.mult), reads=[VEC[:]], writes=[VEC[:]])
        op("dve", lambda e: e.tensor_scalar(out=nl2, in0=nl, scalar1=-2.0 * LRU_C, scalar2=None, op0=ALU.mult),
           reads=[VEC[:]], writes=[VEC[:]])
        op("dve", lambda e: e.tensor_scalar(out=nl, in0=nl, scalar1=-LRU_C, scalar2=None, op0=ALU.mult),
           reads=[VEC[:]], writes=[VEC[:]])

        def mm_group(ps_ap, lhs, rhs):
            n = len(lhs)
            for i in range(n):
                op("pe", lambda e, i=i: e.matmul(ps_ap, lhsT=lhs[i], rhs=rhs[i], start=(i == 0), stop=(i == n - 1)),
                   reads=[lhs[i], rhs[i]], writes=[ps_ap])

        def tks(tile):
            return slice(tile * NT, (tile + 1) * NT)

        def rstd_from(ps, d_inv):
            rs = RSTD.get()[:, 0:NT]
            op("act", lambda e: e.activation(out=rs[:], in_=ps[:], func=AF.Ln, scale=d_inv, bias=vc(VC_EPS)),
               reads=[ps[:], VEC[:]], writes=[rs[:]])
            op("act", lambda e: e.activation(out=rs[:], in_=rs[:], func=AF.Exp, scale=-0.5), reads=[rs[:]], writes=[rs[:]])
            return rs

        def sq_accum(pst, src, c, defer=None, sqrot=None):
            sq = (sqrot or SQR).get()
            op("act", lambda e: e.activation(out=sq[:], in_=src, func=AF.Square), reads=[src], writes=[sq[:]])

            def pe_half():
                op("pe", lambda e: e.matmul(pst[:], lhsT=ONES[:], rhs=sq[:], start=(c == 0), stop=(c == KC - 1)),
                   reads=[ONES[:], sq[:]], writes=[pst[:]])
            if defer is None:
                pe_half()
            else:
                defer.append(pe_half)
                while len(defer) > 2:
                    defer.pop(0)()

        def xn_from_H(l, which, tile, rs):
            tk = tks(tile)
            for c in range(KC):
                g = vc(VC_NORM(l, which, c))
                op("dve", lambda e, c=c, g=g: e.scalar_tensor_tensor(
                    out=XNt(c, tile), in0=H[:, c, tk], scalar=g, in1=rs[:], op0=ALU.mult, op1=ALU.mult),
                   reads=[H[:, c, tk], VEC[:], rs[:]], writes=[XNt(c, tile)])

        def pre_norm(l, which):
            for tile in range(NTILE):
                tk = tks(tile)
                pst = PS.get()
                for c in range(KC):
                    sq_accum(pst, H[:, c, tk], c)
                rs = rstd_from(pst, 1.0 / D)
                xn_from_H(l, which, tile, rs)

        PEND = []

        def prenorm_gen(l, which, tile):
            tk = tks(tile)
            hh = PS.hold(1)
            pst = PS.b[hh[0]]
            for c in range(KC):
                sq_accum(pst, H[:, c, tk], c, defer=PEND)
                yield
            while PEND:
                PEND.pop(0)()
            rs = rstd_from(pst, 1.0 / D)
            PS.release(hh)
            xn_from_H(l, which, tile, rs)

        def flush_pend():
            while PEND:
                PEND.pop(0)()

        def proj_gen(units, nq, nk, rhs_of, l, which_post, tile, pst):
            for (g, W) in units:
                for q in range(nq):
                    dc = g * nq + q
                    ps = PS.get()
                    mm_group(ps[:], [W(k, q) for k in range(nk)], [rhs_of(k, tile) for k in range(nk)])
                    flush_pend()
                    sq_accum(pst, ps[:], dc, defer=PEND)
                    gp = vc(VC_NORM(l, which_post, dc))
                    op("act", lambda e, ps=ps, dc=dc, gp=gp: e.activation(out=MFt(dc, tile), in_=ps[:], func=AF.Copy, scale=gp),
                       reads=[ps[:], VEC[:]], writes=[MFt(dc, tile)])
                    yield

        def site_gen(pst, tile, nxt, release=None, sqrot=None):
            tk = tks(tile)
            flush_pend()
            rs = rstd_from(pst, 1.0 / D)
            h2 = PS.hold(1) if nxt is not None else None
            ps2 = PS.b[h2[0]] if nxt is not None else None
            for c in range(KC):
                op("dve", lambda e, c=c: e.tensor_tensor(out=MFt(c, tile), in0=MFt(c, tile), in1=rs[:], op=ALU.mult),
                   reads=[MFt(c, tile), rs[:]], writes=[MFt(c, tile)])
                op("dve", lambda e, c=c: e.tensor_tensor(out=H[:, c, tk], in0=H[:, c, tk], in1=MFt(c, tile), op=ALU.add),
                   reads=[H[:, c, tk], MFt(c, tile)], writes=[H[:, c, tk]])
                if nxt is not None:
                    sq_accum(ps2, H[:, c, tk], c, defer=PEND, sqrot=sqrot)
                yield
            flush_pend()
            if nxt is not None:
                rs2 = rstd_from(ps2, 1.0 / D)
                PS.release(h2)
                xn_from_H(nxt[0], nxt[1], tile, rs2)
            if release is not None:
                PS.release(release)

        def run(gen):
            for _ in gen:
                pass

        def lockstep(gens, offsets=None):
            gens = list(gens)
            offs = list(offsets) if offsets is not None else [0] * len(gens)
            live = list(zip(offs, gens))
            rnd = 0
            while live:
                nxt_live = []
                for o, g_ in live:
                    if o > rnd:
                        nxt_live.append((o, g_))
                        continue
                    try:
                        next(g_)
                        nxt_live.append((o, g_))
                    except StopIteration:
                        pass
                live = nxt_live
                rnd += 1

        def pipeline(items, stageA, stageB):
            ctx = []
            for i, it in enumerate(items):
                ctx.append(stageA(it))
                if i >= 1:
                    stageB(items[i - 1], ctx[i - 1])
            if items:
                stageB(items[-1], ctx[-1])

        def ffn(l, nxt, pending, before_down=None):
            def gate_steps(Wg, g, nq, tile):
                tk = tks(tile)
                for q in range(nq):
                    hc = g * 4 + q
                    ps = PS.get()
                    mm_group(ps[:], [Wg[:, k, q * P:(q + 1) * P] for k in range(KC)], [XNt(k, tile) for k in range(KC)])
                    flush_pend()
                    op("act", lambda e, ps=ps, hc=hc: e.activation(out=A[:, hc, tk], in_=ps[:], func=AF.Silu),
                       reads=[ps[:]], writes=[A[:, hc, tk]])
                    yield

            def up_steps(Wu, g, nq, tile):
                tk = tks(tile)
                for q in range(nq):
                    hc = g * 4 + q
                    ps = PS.get()
                    mm_group(ps[:], [Wu[:, k, q * P:(q + 1) * P] for k in range(KC)], [XNt(k, tile) for k in range(KC)])
                    flush_pend()
                    op("dve", lambda e, ps=ps, hc=hc: e.tensor_tensor(out=A[:, hc, tk], in0=A[:, hc, tk], in1=ps[:], op=ALU.mult),
                       reads=[A[:, hc, tk], ps[:]], writes=[A[:, hc, tk]])
                    yield

            def chain(*gens):
                for g_ in gens:
                    for _ in g_:
                        yield

            Wg = k3(WR.next(("gate", 0)), KC)
            Wu = k3(WR.next(("up", 0), live=1), KC)
            Wg1 = k3(WR.next(("gate", 1), live=2), KC)
            lockstep([pending, chain(gate_steps(Wg, 0, 4, 0), up_steps(Wu, 0, 4, 0), gate_steps(Wg1, 1, 4, 0))])
            run(chain(gate_steps(Wg, 0, 4, 1), up_steps(Wu, 0, 4, 1), gate_steps(Wg1, 1, 4, 1)))
            Wu1 = k3(WR.next(("up", 1)), KC)
            for tile in range(NTILE):
                run(up_steps(Wu1, 1, 4, tile))
            for g in range(2, 6):
                nq = 4 if g < 5 else 2
                Wg = k3(WR.next(("gate", g)), KC)
                for tile in range(NTILE):
                    run(gate_steps(Wg, g, nq, tile))
                Wu = k3(WR.next(("up", g)), KC)
                for tile in range(NTILE):
                    run(up_steps(Wu, g, nq, tile))
            if before_down is not None:
                before_down()
            S.mark("  down")
            held = PS.hold(NTILE)
            pst = [PS.b[k] for k in held]
            rhs_of = lambda k, tile: A[:, k, tks(tile)]
            def down_unit(g, live0):
                Wa = k3(WR.next(("down", g, 0), live=live0), 11)
                Wb = k3(WR.next(("down", g, 1), live=live0 + 1), 11)
                return (g, lambda k, q, Wa=Wa, Wb=Wb: (Wa if k < 11 else Wb)[:, k % 11, q * P:(q + 1) * P])

            for g in range(2):
                u_ = down_unit(g, 0)
                for tile in range(NTILE):
                    run(proj_gen([u_], 2, HC, rhs_of, l, 3, tile, pst[tile]))
            units = [down_unit(2, 0), down_unit(3, 2)]
            run(proj_gen(units, 2, HC, rhs_of, l, 3, 0, pst[0]))
            S.mark("  postnorm")
            lockstep([site_gen(pst[0], 0, nxt), proj_gen(units, 2, HC, rhs_of, l, 3, 1, pst[1])])
            return site_gen(pst[1], 1, nxt, release=held, sqrot=Rot(SQR.items[1:4]))

        def w_out_phase(l):
            held = PS.hold(NTILE)
            pst = [PS.b[k] for k in held]
            rhs_of = lambda k, tile: Y[:, k, tks(tile)]
            def wout_unit(g, live0):
                W = k3(WR.next(("wout", g), live=live0), KC)
                return (g, lambda k, q, W=W: W[:, k, q * P:(q + 1) * P])

            units = [wout_unit(0, 0), wout_unit(1, 1)]
            run(proj_gen(units, 4, KC, rhs_of, l, 1, 0, pst[0]))
            S.mark("  postnorm")
            lockstep([site_gen(pst[0], 0, (l, 2)), proj_gen(units, 4, KC, rhs_of, l, 1, 1, pst[1])])
            return site_gen(pst[1], 1, (l, 2), release=held)

        def even_mixer(l, seq_first):
            items = [(j, tile) for j in range(4) for tile in range(NTILE)]
            cur = {}

            TF = FreeList(TMP.items)

            def lru_item(it):
                j, tile = it
                tk = tks(tile)
                if "xa" not in cur:
                    cur["xa"] = k3(WR.next(("ein", "xa")), KC)
                    cur["ga"] = k3(WR.next(("ein", "ga"), live=1), KC)
                kx, psx = PS.alloc()
                mm_group(psx[:], [cur["xa"][:, k, j * P:(j + 1) * P] for k in range(KC)], [XNt(k, tile) for k in range(KC)])
                yield
                XA = TF.alloc()
                op("dve", lambda e: e.tensor_copy(out=XA[:, 0:3], in_=CARX[:, j, 0:3]), reads=[CARX[:, j, :]], writes=[XA[:, 0:3]])
                op("act", lambda e: e.activation(out=XA[:, 3:3 + NT], in_=psx[:], func=AF.Copy), reads=[psx[:]], writes=[XA[:, 3:3 + NT]])
                PS.free(kx)
                op("dve", lambda e: e.tensor_copy(out=CARX[:, j, 0:3], in_=XA[:, NT:NT + 3]), reads=[XA[:, NT:NT + 3]], writes=[CARX[:, j, :]])
                yield
                XC = TF.alloc()
                op("dve", lambda e: e.tensor_scalar(out=XC[:, 0:NT], in0=XA[:, 0:NT], scalar1=vc(VC_CONVW + j), scalar2=vc(VC_CONVB + j),
                                                    op0=ALU.mult, op1=ALU.add),
                   reads=[XA[:, 0:NT], VEC[:]], writes=[XC[:, 0:NT]])
                for k in range(1, 4):
                    op("dve", lambda e, k=k: e.scalar_tensor_tensor(out=XC[:, 0:NT], in0=XA[:, k:k + NT], scalar=vc(VC_CONVW + 4 * k + j),
                                                                   in1=XC[:, 0:NT], op0=ALU.mult, op1=ALU.add),
                       reads=[XA[:, k:k + NT], XC[:, 0:NT], VEC[:]], writes=[XC[:, 0:NT]])
                yield
                XCB = TMPB.get()
                op("act", lambda e: e.activation(out=XCB[:], in_=XC[:, 0:NT], func=AF.Copy), reads=[XC[:, 0:NT]], writes=[XCB[:]])
                kr, psr = PS.alloc()
                mm_group(psr[:], [WRI[:, j, :]], [XCB[:]])
                ki, psi = PS.alloc()
                mm_group(psi[:], [WRI[:, 4 + j, :]], [XCB[:]])
                yield
                R = TF.alloc()
                I = TF.alloc()
                op("act", lambda e: e.activation(out=R[:, 0:NT], in_=psr[:], func=AF.Sigmoid, bias=vc(VC_BR + j)),
                   reads=[psr[:], VEC[:]], writes=[R[:, 0:NT]])
                op("act", lambda e: e.activation(out=I[:, 0:NT], in_=psi[:], func=AF.Sigmoid, bias=vc(VC_BI + j)),
                   reads=[psi[:], VEC[:]], writes=[I[:, 0:NT]])
                PS.free(kr)
                PS.free(ki)
                op("dve", lambda e: e.tensor_tensor(out=I[:, 0:NT], in0=I[:, 0:NT], in1=XC[:, 0:NT], op=ALU.mult),
                   reads=[I[:, 0:NT], XC[:, 0:NT]], writes=[I[:, 0:NT]])
                yield
                AA = XA
                A2 = XC
                op("act", lambda e: e.activation(out=AA[:, 0:NT], in_=R[:, 0:NT], func=AF.Exp, scale=vc(VC_NL + j)),
                   reads=[R[:, 0:NT], VEC[:]], writes=[AA[:, 0:NT]])
                op("act", lambda e: e.activation(out=A2[:, 0:NT], in_=R[:, 0:NT], func=AF.Exp, scale=vc(VC_NL2 + j)),
                   reads=[R[:, 0:NT], VEC[:]], writes=[A2[:, 0:NT]])
                yield
                op("act", lambda e: e.activation(out=A2[:, 0:NT], in_=A2[:, 0:NT], func=AF.Sqrt, scale=-1.0, bias=vc(VC_ONE)),
                   reads=[A2[:, 0:NT], VEC[:]], writes=[A2[:, 0:NT]])
                op("dve", lambda e: e.tensor_tensor(out=I[:, 0:NT], in0=I[:, 0:NT], in1=A2[:, 0:NT], op=ALU.mult),
                   reads=[I[:, 0:NT], A2[:, 0:NT]], writes=[I[:, 0:NT]])
                HS = R
                op("dve", lambda e: e.tensor_tensor_scan(out=HS[:, 0:NT], data0=AA[:, 0:NT], data1=I[:, 0:NT], initial=CARH[:, j:j + 1],
                                                         op0=ALU.mult, op1=ALU.add),
                   reads=[AA[:, 0:NT], I[:, 0:NT], CARH[:, j:j + 1]], writes=[HS[:, 0:NT]])
                op("dve", lambda e: e.tensor_copy(out=CARH[:, j:j + 1], in_=HS[:, NT - 1:NT]), reads=[HS[:, NT - 1:NT]], writes=[CARH[:, j:j + 1]])
                TF.free(XC)
                TF.free(I)
                kg, psg = PS.alloc()
                mm_group(psg[:], [cur["ga"][:, k, j * P:(j + 1) * P] for k in range(KC)], [XNt(k, tile) for k in range(KC)])
                yield
                GE = AA
                op("act", lambda e: e.activation(out=GE[:, 0:NT], in_=psg[:], func=AF.Gelu_apprx_tanh), reads=[psg[:]], writes=[GE[:, 0:NT]])
                PS.free(kg)
                op("dve", lambda e: e.tensor_tensor(out=Y[:, j, tk], in0=HS[:, 0:NT], in1=GE[:, 0:NT], op=ALU.mult),
                   reads=[HS[:, 0:NT], GE[:, 0:NT]], writes=[Y[:, j, tk]])
                TF.free(XA)
                TF.free(R)

            def sc_tile(tile):
                tk = tks(tile)
                if "cg" not in cur:
                    cur["cg"] = k3(WR.next(("ein", "cg"), live=2), KC)
                    cur["hs"] = k3(WR.next(("ein", "hs"), live=3), KC)
                    cur["bg"] = k3(WR.next(("ein", "bg"), live=4), KC)
                if tile == 0:
                    CG = RSTD.items[0]
                    PP = RSTD.items[1]
                else:
                    CG = BIGF[:, 4096:4096 + NT + 8]
                    PP = BIGF[:, 4096 + 544:4096 + 544 + NT + 8]

                def mm(nm, j):
                    k_, ps = PS.alloc()
                    mm_group(ps[:], [cur[nm][:, k, j * P:(j + 1) * P] for k in range(KC)], [XNt(k, tile) for k in range(KC)])
                    return k_, ps

                for j in range(4):
                    kc_, psc = mm("cg", j)
                    yield
                    op("act", lambda e, psc=psc: e.activation(out=CG[:, 0:NT], in_=psc[:], func=AF.Copy), reads=[psc[:]], writes=[CG[:, 0:NT]])
                    PS.free(kc_)
                    kh_, psh = mm("hs", j)
                    yield
                    op("dve", lambda e, j=j: e.tensor_copy(out=PP[:, 0:2], in_=CARP[:, j, 0:2]), reads=[CARP[:, j, :]], writes=[PP[:, 0:2]])
                    op("dve", lambda e, psh=psh: e.tensor_tensor(out=PP[:, 2:2 + NT], in0=CG[:, 0:NT], in1=psh[:], op=ALU.mult),
                       reads=[CG[:, 0:NT], psh[:]], writes=[PP[:, 2:2 + NT]])
                    op("dve", lambda e, j=j: e.tensor_copy(out=CARP[:, j, 0:2], in_=PP[:, NT:NT + 2]), reads=[PP[:, NT:NT + 2]], writes=[CARP[:, j, :]])
                    PS.free(kh_)
                    kb_, psb = mm("bg", j)
                    yield
                    CV = CG
                    op("dve", lambda e, j=j: e.tensor_scalar(out=CV[:, 0:NT], in0=PP[:, 0:NT], scalar1=vc(VC_SCW + j), scalar2=None, op0=ALU.mult),
                       reads=[PP[:, 0:NT], VEC[:]], writes=[CV[:, 0:NT]])
                    for k in range(1, 3):
                        op("dve", lambda e, k=k, j=j: e.scalar_tensor_tensor(out=CV[:, 0:NT], in0=PP[:, k:k + NT], scalar=vc(VC_SCW + 4 * k + j),
                                                                            in1=CV[:, 0:NT], op0=ALU.mult, op1=ALU.add),
                           reads=[PP[:, k:k + NT], CV[:, 0:NT], VEC[:]], writes=[CV[:, 0:NT]])
                    op("dve", lambda e, j=j, psb=psb: e.tensor_tensor(out=Y[:, 4 + j, tk], in0=CV[:, 0:NT], in1=psb[:], op=ALU.mult),
                       reads=[CV[:, 0:NT], psb[:]], writes=[Y[:, 4 + j, tk]])
                    PS.free(kb_)
                    yield

            def chain_gens(*gens):
                for g_ in gens:
                    for _ in g_:
                        yield

            nsteps = LRU_T1
            gens = [lru_item((j, 0)) for j in range(4)] + [lru_item((j, 1)) for j in range(4)]
            offs = tuple(LRU_OFFS) + tuple(o + nsteps for o in LRU_OFFS)
            lockstep(gens + [sc_tile(0), sc_tile(1)], offsets=offs + SC_OFFS)
            S.mark('  sc')
            S.mark('  wout')
            return w_out_phase(l)

        def odd_mixer(l, seq_first, pending=None):
            items = [(g, tile) for g in range(4) for tile in range(NTILE)]
            cur = {}

            TMPP = Rot(TMP.items[0:13])

            def pool_item(it):
                g, tile = it
                tk = tks(tile)
                win = WINS[g]
                if "w" not in cur:
                    cur["w"] = k3(WR.next(("oin", "xp")), KC)
                Wp = cur["w"]
                kps, ps = PS.alloc()
                mm_group(ps[:], [Wp[:, k, g * P:(g + 1) * P] for k in range(KC)], [XNt(k, tile) for k in range(KC)])
                yield
                XP = TMPP.get()
                LA = TMPP.get()
                LB = TMPP.get() if g >= 1 else None
                op("dve", lambda e: e.tensor_copy(out=XP[:, 1:16], in_=CARQ[:, g, 0:15]), reads=[CARQ[:, g, :]], writes=[XP[:, 1:16]])
                op("act", lambda e: e.activation(out=XP[:, 16:16 + NT], in_=ps[:], func=AF.Copy), reads=[ps[:]], writes=[XP[:, 16:16 + NT]])
                PS.free(kps)
                op("dve", lambda e: e.tensor_copy(out=CARQ[:, g, 0:15], in_=XP[:, NT + 1:NT + 16]), reads=[XP[:, NT + 1:NT + 16]], writes=[CARQ[:, g, :]])
                yield
                L = XP
                for lev in range(g + 1):
                    sh = 1 << lev
                    v = 2 * sh
                    NL = LA if lev % 2 == 0 else LB
                    op("dve", lambda e, L=L, NL=NL, v=v, sh=sh: e.tensor_tensor(out=NL[:, v:16 + NT], in0=L[:, v:16 + NT],
                                                                                in1=L[:, v - sh:16 + NT - sh], op=ALU.add),
                       reads=[L[:, v - sh:16 + NT]], writes=[NL[:, v:16 + NT]])
                    L = NL
                    yield
                PB = TMPB.get()
                op("dve", lambda e: e.scalar_tensor_tensor(out=PB[:], in0=L[:, 16:16 + NT], scalar=1.0 / win, in1=XP[:, 16:16 + NT],
                                                           op0=ALU.mult, op1=ALU.subtract),
                   reads=[L[:, 16:16 + NT], XP[:, 16:16 + NT]], writes=[PB[:]])
                if seq_first and tile == 0:
                    T1 = STAT[:, g, 0:16]
                    op("dve", lambda e: e.tensor_tensor(out=T1, in0=L[:, 16:32], in1=INVC[:, g, :], op=ALU.mult),
                       reads=[L[:, 16:32], INVC[:]], writes=[T1])
                    op("dve", lambda e: e.tensor_tensor(out=PB[:, 0:16], in0=T1, in1=XP[:, 16:32], op=ALU.subtract),
                       reads=[T1, XP[:, 16:32]], writes=[PB[:, 0:16]])
                kpy, psy = PS.alloc()
                mm_group(psy[:], [POOLW[:, g, :]], [PB[:]])
                yield
                op("act", lambda e: e.activation(out=Y[:, g, tk], in_=psy[:], func=AF.Copy, scale=vc(VC_PSC + g)),
                   reads=[psy[:], VEC[:]], writes=[Y[:, g, tk]])
                PS.free(kpy)

            def spatial_tile(tile, blocks=(0, 1, 2, 3), chain=None):
                if "v" not in cur:
                    cur["v"] = k3(WR.next(("oin", "v"), live=1), KC)
                    cur["u"] = k3(WR.next(("oin", "u"), live=2), KC)
                Wv, Wu = cur["v"], cur["u"]
                vn_i = tile if chain is None else chain
                for b in blocks:
                    t0 = tile * NT + b * P
                    xb = [XNt(k, tile)[:, b * P:(b + 1) * P] for k in range(KC)]
                    kv, psv = PS.alloc()
                    mm_group(psv[:], xb, [Wv[:, k, :] for k in range(KC)])
                    yield
                    si = b % 4 + 4
                    VN = SQR.items[vn_i]
                    JK = VN
                    ssq = STAT[:, si, 2 * tile:2 * tile + 1]
                    rt = STAT[:, si, 2 * tile + 1:2 * tile + 2]
                    op("act", lambda e, JK=JK, psv=psv, ssq=ssq: e.activation(out=JK[:], in_=psv[:], func=AF.Square, accum_out=ssq),
                       reads=[psv[:]], writes=[JK[:], ssq])
                    op("act", lambda e, ssq=ssq, rt=rt: e.activation(out=rt, in_=ssq, func=AF.Sqrt, scale=1.0 / 512.0, bias=vc(VC_EPS)),
                       reads=[ssq, VEC[:]], writes=[rt])
                    op("dve", lambda e, rt=rt: e.reciprocal(out=rt, in_=rt), reads=[rt], writes=[rt])
                    op("dve", lambda e, VN=VN, psv=psv, rt=rt: e.scalar_tensor_tensor(out=VN[:], in0=psv[:], scalar=rt, in1=SGN[:],
                                                                                   op0=ALU.mult, op1=ALU.mult),
                       reads=[psv[:], rt, SGN[:]], writes=[VN[:]])
                    PS.free(kv)
                    yield
                    kg, psg = PS.alloc()
                    for h in range(8):
                        o = psg[(h % 2) * 64:(h % 2) * 64 + 64, (h // 2) * P:(h // 2 + 1) * P]
                        op("pe", lambda e, o=o, VN=VN, h=h: e.matmul(o, lhsT=VN[:, h * 64:(h + 1) * 64], rhs=SGW[:, h, :], start=True, stop=True),
                           reads=[VN[:], SGW[:, h, :]], writes=[o])
                    ku, psu = PS.alloc()
                    for j in range(4):
                        mm_group(psu[:, j * P:(j + 1) * P], [Wu[:, k, j * P:(j + 1) * P] for k in range(KC)], xb)
                    yield
                    T1 = TMP.items[13 + vn_i]
                    t1v = T1[:, 0:NT].rearrange("p (j t) -> p j t", j=4)
                    pgv = psg[:].rearrange("p (j t) -> p j t", j=4)
                    puv = psu[:].rearrange("p (j t) -> p j t", j=4)
                    yo = Y[:, 4:8, t0:t0 + P]
                    op("dve", lambda e, t1v=t1v, pgv=pgv: e.tensor_tensor(out=t1v, in0=pgv, in1=SGB[:], op=ALU.add),
                       reads=[psg[:], SGB[:]], writes=[T1[:, 0:NT]])
                    op("dve", lambda e, t1v=t1v, puv=puv, yo=yo: e.tensor_tensor(out=yo, in0=t1v, in1=puv, op=ALU.mult),
                       reads=[T1[:, 0:NT], psu[:]], writes=[yo])
                    PS.free(kg)
                    PS.free(ku)
                    yield

            def chain_gens(*gens):
                for g_ in gens:
                    for _ in g_:
                        yield

            gens = [pool_item((g, 0)) for g in (3, 2, 1, 0)] + [pool_item((g, 1)) for g in (3, 2, 1, 0)]
            offs = (0, 0, 1, 1, 9, 9, 10, 10)
            pre = [pending] if pending is not None else []
            sp = [spatial_tile(0, chain=0), spatial_tile(1, (0, 1), chain=1), spatial_tile(1, (2, 3), chain=2)]
            lockstep(pre + gens + sp, offsets=(0,) * len(pre) + offs + SP_OFFS)
            S.mark('  spatial')
            S.mark('  wout')
            return w_out_phase(l)

        STG_HI = [BIGF[:, (4 + i) * 1024:(5 + i) * 1024] for i in range(NSTG)]
        STG_LO = [BIGF[:, i * 1024:(i + 1) * 1024] for i in range(NSTG)]
        stg_cnt = {"x": 0, "o": 0}

        xstage = {}
        xsem16 = [S.dma_sem(f"xs{i}") for i in range(16)]

        def issue_loads(ps_, blks):
            t0 = ps_ * T
            for blk in blks:
                for half in range(2):
                    i = (2 * blk + half) % 16
                    xt = TMP.items[i]
                    xstage[(ps_, blk, half)] = xt
                    op("sp", lambda e, xt=xt, blk=blk, half=half: e.dma_start(
                        out=xt[:, 0:512], in_=x[t0 + blk * P:t0 + (blk + 1) * P, half * 512:(half + 1) * 512]),
                       writes=[xt[:, 0:512]], dma=xsem16[i])

        def load_gen(ps_, blks):
            for blk in blks:
                if (ps_, blk, 0) not in xstage:
                    issue_loads(ps_, [blk])
                for half in range(2):
                    xt = xstage.pop((ps_, blk, half))
                    ps = PS.get()
                    for q in range(4):
                        op("pe", lambda e, ps=ps, q=q, xt=xt: e.transpose(out=ps[:, q * P:(q + 1) * P], in_=xt[:, q * P:(q + 1) * P], identity=IDENT[:]),
                           reads=[xt[:, q * P:(q + 1) * P], IDENT[:]], writes=[ps[:, q * P:(q + 1) * P]])
                    dst = H[:, half * 4:half * 4 + 4, blk * P:(blk + 1) * P]
                    src = ps[:].rearrange("p (q t) -> p q t", q=4)
                    if half == 0:
                        op("act", lambda e, dst=dst, src=src: e.activation(out=dst, in_=src, func=AF.Copy), reads=[ps[:]], writes=[dst])
                    else:
                        op("dve", lambda e, dst=dst, src=src: e.tensor_copy(out=dst, in_=src), reads=[ps[:]], writes=[dst])
                yield

        out_toks = []

        def store_gen(ps_, blks, stg):
            t0 = ps_ * T
            for blk in blks:
                i = stg_cnt["o"] % NSTG
                stg_cnt["o"] += 1
                ot = stg[i]
                for half in range(2):
                    ps = PS.get()
                    for q in range(4):
                        c = half * 4 + q
                        src = H[:, c, blk * P:(blk + 1) * P]
                        op("pe", lambda e, ps=ps, q=q, src=src: e.transpose(out=ps[:, q * P:(q + 1) * P], in_=src, identity=IDENT[:]),
                           reads=[src, IDENT[:]], writes=[ps[:, q * P:(q + 1) * P]])
                    dst = ot[:, half * 512:(half + 1) * 512]
                    if half == 0:
                        op("act", lambda e, dst=dst, ps=ps: e.activation(out=dst, in_=ps[:], func=AF.Copy), reads=[ps[:]], writes=[dst])
                    else:
                        op("dve", lambda e, dst=dst, ps=ps: e.tensor_copy(out=dst, in_=ps[:]), reads=[ps[:]], writes=[dst])
                tok = op("pool", lambda e, ot=ot, blk=blk: e.dma_start(out=out[t0 + blk * P:t0 + (blk + 1) * P, :], in_=ot),
                         reads=[ot], dma=osem[i])
                out_toks.append(tok)
                yield

        def mark(name):
            PHASE_MARKS.append((name, len(S.ops["pe"])))

        S.mark = mark
        NB = T // P // NTILE
        carry_site = None
        for ps_ in range(NPASS):
            mark(f"p{ps_} load")
            if ps_ == 0:
                issue_loads(0, range(0, 2 * NB))
                S.wait_all("pool", [(xsem16[i], S.semcnt[xsem16[i]]) for i in range(8)])
            run(load_gen(ps_, range(0, NB)))
            lockstep([prenorm_gen(layers[0], 0, 0), load_gen(ps_, range(NB, 2 * NB))])
            run(prenorm_gen(layers[0], 0, 1))
            for li, l in enumerate(layers):
                mark(f"p{ps_} L{l} mixer")
                if l == 0:
                    pending = even_mixer(l, ps_ == 0)
                else:
                    pending = odd_mixer(l, ps_ == 0, carry_site)
                mark(f"p{ps_} L{l} ffn")
                nxt = (layers[li + 1], 0) if li + 1 < len(layers) else None
                pre = None
                if nxt is None and ps_ + 1 < NPASS:
                    pre = lambda ps_=ps_: issue_loads(ps_ + 1, range(0, 2 * NB))
                pending = ffn(l, nxt, pending, pre)
                carry_site = None
                if nxt is not None and nxt[0] == 1:
                    carry_site = pending
                else:
                    run(pending)
            mark(f"p{ps_} store")
            run(store_gen(ps_, range(0, 2 * NB), STG_HI))
        last = {}
        for k, v in out_toks:
            last[k] = max(last.get(k, 0), v)
        S.wait_all("sp", list(last.items()))
        S.emit()
    return nc


def _host_layout(inp):
    f = np.float32
    vec = np.zeros((P, P), f)

    def put(col, v):
        v = np.asarray(v, f).reshape(-1, P)
        vec[:, col:col + v.shape[0]] = v.T

    for l in range(2):
        put(VC_NORM(l, 0, 0), inp["norm_mix_pre"][l])
        put(VC_NORM(l, 1, 0), inp["norm_mix_post"][l])
        put(VC_NORM(l, 2, 0), inp["norm_ffn_pre"][l])
        put(VC_NORM(l, 3, 0), inp["norm_ffn_post"][l])
    for k in range(4):
        put(VC_CONVW + 4 * k, inp["e_conv_w"][0, k])
    put(VC_CONVB, inp["e_conv_b"][0])
    put(VC_BR, inp["e_b_r"][0].reshape(-1))
    put(VC_BI, inp["e_b_i"][0].reshape(-1))
    put(VC_LAM, inp["e_lam"][0])
    for k in range(3):
        put(VC_SCW + 4 * k, inp["e_sc_conv_w"][0, k])
    put(VC_PSC, inp["o_pool_scale"][0])
    wri = np.zeros((P, 8, P), f)
    for which, w in enumerate((inp["e_w_r"][0], inp["e_w_i"][0])):
        for h in range(8):
            j, o = h // 2, (h % 2) * 64
            wri[o:o + 64, which * 4 + j, o:o + 64] = w[h]
    sgwT = np.ascontiguousarray(np.transpose(np.asarray(inp["o_sg_w"][0], f), (2, 0, 1)))
    sb = np.asarray(inp["o_sg_b"][0], f)
    sgb = np.ascontiguousarray(np.repeat(sb.reshape(4, 2, 1, P), 64, axis=2).reshape(4, P, P).transpose(1, 0, 2))
    sgn = np.ascontiguousarray(np.broadcast_to(np.asarray(inp["o_sg_norm"][0], f)[None, :], (P, 512)))
    pw = np.ascontiguousarray(np.transpose(np.asarray(inp["o_pool_w"][0], f), (1, 0, 2)))
    invc = np.zeros((P, 4, 16), f)
    for g, win in enumerate(WINS):
        invc[:, g, :] = 1.0 / np.minimum(np.arange(1, 17), win).astype(f)
    return {
        "e_w_in": np.ascontiguousarray(inp["e_w_in"][0], f), "e_w_out": np.ascontiguousarray(inp["e_w_out"][0], f),
        "o_w_in": np.ascontiguousarray(inp["o_w_in"][0], f), "o_w_out": np.ascontiguousarray(inp["o_w_out"][0], f),
        "w_gate": np.ascontiguousarray(inp["w_gate"], f), "w_up": np.ascontiguousarray(inp["w_up"], f),
        "w_down": np.ascontiguousarray(inp["w_down"], f),
        "pool_w": pw, "wri": wri, "sgwT": sgwT, "sgb": sgb, "sgn": sgn, "vecs": vec, "invc": invc,
    }


def kernel(**inputs):
    inp = {k: np.asarray(v) for k, v in inputs.items()}
    shared = _host_layout(inp)
    x = np.ascontiguousarray(inp["x"], np.float32)
    nc = build_program((0, 1))
    in_maps = [dict(shared, x=x[b]) for b in range(8)]
    res = run_bass_kernel_spmd(nc, in_maps, core_ids=list(range(8)))
    return np.stack([np.asarray(r["out"], np.float32) for r in res.results], axis=0)
```

```python
import numpy as np
from contextlib import ExitStack
import concourse.bass as bass
import concourse.mybir as mybir
from concourse.bass_utils import run_bass_kernel_spmd

F32 = mybir.dt.float32
BF16 = mybir.dt.bfloat16
AF = mybir.ActivationFunctionType
ALU = mybir.AluOpType
GRAN = 128

P = 128
D = 1024
KC = 8
SEQ = 2048
T = 1024
NT = 512
NTILE = T // NT
NPASS = SEQ // T
HID = 2816
HC = HID // P
EPS = 1e-6
LRU_C = 8.0
WINS = (2, 4, 8, 16)
TMPW = 528
LRU_OFFS = (0, 0, 3, 3)
SP_OFFS = (2, 9, 11)
SC_OFFS = (1, 6)
SLOT_COLS = 4096
RING_COLS = 5 * SLOT_COLS


def _esize(dt):
    if dt == F32:
        return 4
    if dt == BF16:
        return 2
    return mybir.dt.size(dt)


class Sched:
    ENG = ("pe", "act", "dve", "pool", "sp")

    def __init__(self, nc, stack):
        self.nc = nc
        self.stack = stack
        self.ops = {e: [] for e in self.ENG}
        self.sems = {}
        for e in self.ENG:
            self.sems["es_" + e] = stack.enter_context(nc.semaphore("es_" + e))
        self.ecnt = {e: 0 for e in self.ENG}
        self.semcnt = {}
        self.waited = {e: {} for e in self.ENG}
        self.lastw = {}
        self.readers = {}
        self.base = {}

    def _pad_to_gran(self, end):
        nxt = (end + 31) // 32 * 32
        if nxt % GRAN:
            self.npad = getattr(self, "npad", 0) + 1
            p = self.stack.enter_context(self.nc.sbuf_tensor(f"pad{self.npad}", [128, (GRAN - nxt % GRAN) // 2], BF16))
            assert int(self.nc.lookup_mloc(p).addr) == nxt, (int(self.nc.lookup_mloc(p).addr), nxt)

    def sbuf(self, name, shape, dt):
        if not self.base:
            p0 = self.stack.enter_context(self.nc.sbuf_tensor("pad_first", [128, 16], BF16))
            self._pad_to_gran(int(self.nc.lookup_mloc(p0).addr) + 32)
        t = self.stack.enter_context(self.nc.sbuf_tensor(name, list(shape), dt))
        addr = int(self.nc.lookup_mloc(t).addr)
        assert addr % GRAN == 0, (name, addr)
        self.base[name] = (0, addr)
        size = _esize(dt)
        for d in list(shape)[1:]:
            size *= int(d)
        self._pad_to_gran(addr + size)
        return t

    def psum(self, name, shape, dt):
        t = self.stack.enter_context(self.nc.psum_tensor(name, list(shape), dt))
        ml = self.nc.lookup_mloc(t)
        self.base[name] = (1, int(ml.bank) * 2048 + int(ml.addr))
        return t

    def dma_sem(self, name):
        self.sems[name] = self.stack.enter_context(self.nc.semaphore(name))
        self.semcnt[name] = 0
        return name

    def _grans(self, ap):
        t = ap.tensor
        nm = t.name
        if nm not in self.base:
            return ()
        space, b0 = self.base[nm]
        row = 1
        for s in list(t.shape)[1:]:
            row *= int(s)
        es = _esize(ap.dtype)
        tes = _esize(t.dtype)
        row_ap = row * tes // es
        off = int(ap.offset) % row_ap
        ext = 1
        for (st, cnt) in list(ap.ap)[1:]:
            ext += abs(int(st)) * (int(cnt) - 1)
        lo = b0 + off * es
        hi = b0 + (off + ext) * es
        return [(space, g) for g in range(lo // GRAN, (hi - 1) // GRAN + 1)]

    def op(self, eng, fn, reads=(), writes=(), dma=None):
        deps = {}
        rg = []
        for ap in reads:
            rg.extend(self._grans(ap))
        wg = []
        for ap in writes:
            wg.extend(self._grans(ap))
        own = "es_" + eng
        own_raw = 0
        track_own = dma is None and eng in ("act", "dve", "pool")
        for g in rg:
            lw = self.lastw.get(g)
            if lw is not None:
                if deps.get(lw[0], 0) < lw[1]:
                    deps[lw[0]] = lw[1]
                if track_own and lw[0] == own and lw[1] > own_raw:
                    own_raw = lw[1]
        for g in wg:
            lw = self.lastw.get(g)
            if lw is not None:
                if deps.get(lw[0], 0) < lw[1]:
                    deps[lw[0]] = lw[1]
                if track_own and lw[0] == own and lw[1] > own_raw:
                    own_raw = lw[1]
            rd = self.readers.get(g)
            if rd:
                for k, v in rd.items():
                    if deps.get(k, 0) < v:
                        deps[k] = v
                    if track_own and k == own and v > own_raw:
                        own_raw = v
        waits = []
        wd = self.waited[eng]
        for k, v in deps.items():
            if k == own:
                continue
            if wd.get(k, 0) >= v:
                continue
            wd[k] = v
            waits.append((k, v))
        if own_raw and wd.get(own, 0) < own_raw:
            wd[own] = own_raw
            waits.append((own, own_raw))
        if dma is None:
            self.ecnt[eng] += 1
            tok = (own, self.ecnt[eng])
            inc = (own, 1)
        else:
            self.semcnt[dma] += 16
            tok = (dma, self.semcnt[dma])
            inc = (dma, 16)
        for g in wg:
            self.lastw[g] = tok
            self.readers[g] = {}
        for g in rg:
            d = self.readers.setdefault(g, {})
            if d.get(tok[0], 0) < tok[1]:
                d[tok[0]] = tok[1]
        self.ops[eng].append((waits, fn, inc))
        return tok

    def wait_all(self, eng, toks):
        self.ops[eng].append((list(toks), None, None))

    def emit(self):
        nc = self.nc
        with nc.Block() as block:
            def mk(ename):
                def body(e):
                    for waits, fn, inc in self.ops[ename]:
                        for k, v in waits:
                            e.wait_ge(self.sems[k], v)
                        if fn is None:
                            continue
                        ins = fn(e)
                        ins.then_inc(self.sems[inc[0]], inc[1])
                return body
            block.tensor(mk("pe"))
            block.scalar(mk("act"))
            block.vector(mk("dve"))
            block.gpsimd(mk("pool"))
            block.sync(mk("sp"))


class Rot:
    def __init__(self, items):
        self.items = items
        self.i = 0

    def get(self):
        x = self.items[self.i % len(self.items)]
        self.i += 1
        return x


class PsumAlloc:
    def __init__(self, S):
        self.b = [S.psum(f"psb{i}", [P, NT], F32) for i in range(8)]
        self.i = 0
        self.held = set()

    def get(self):
        while True:
            k = self.i % 8
            self.i += 1
            if k not in self.held:
                return self.b[k]

    def alloc(self):
        k = self.hold(1)[0]
        return k, self.b[k]

    def free(self, k):
        self.held.discard(k)

    def hold(self, n):
        out = []
        assert len(self.held) + n <= 8, "out of PSUM banks"
        while len(out) < n:
            k = self.i % 8
            self.i += 1
            if k not in self.held:
                self.held.add(k)
                out.append(k)
        return out

    def release(self, ks):
        for k in ks:
            self.held.discard(k)


class WRing:
    def __init__(self, S, ncols, plan, nsem=8):
        self.S = S
        self.ncols = ncols
        self.buf = S.sbuf("wring", [P, ncols], BF16)
        self.sems = [S.dma_sem(f"wrsem{i}") for i in range(nsem)]
        self.plan = plan
        self.pos = []
        self.nslot = ncols // SLOT_COLS
        for i, (tag, n, _) in enumerate(plan):
            assert n <= SLOT_COLS
            self.pos.append(((i % self.nslot) * SLOT_COLS, n))
        self.issued = 0
        self.cur = 0

    def view(self, i):
        off, n = self.pos[i]
        return self.buf[:, off:off + n]

    def _overlaps(self, a, b):
        return a[0] < b[0] + b[1] and b[0] < a[0] + a[1]

    def prime(self, oldest_live):
        while self.issued < len(self.plan):
            u = self.issued
            if u - oldest_live >= len(self.sems):
                break
            if u - self.nslot >= oldest_live:
                break
            v_ = self.view(u)
            sem = self.sems[u % len(self.sems)]
            for dst, src in self.plan[u][2](v_):
                self.S.op("pool", lambda e, dst=dst, src=src: e.dma_start(out=dst, in_=src), writes=[dst], dma=sem)
            self.issued += 1

    def next(self, tag, live=0):
        i = self.cur
        assert self.plan[i][0] == tag, (self.plan[i][0], tag)
        self.cur += 1
        self.prime(i - live)
        assert self.issued > i, ("ring too small for", tag)
        return self.view(i)


def VC_NORM(l, which, c):
    return l * 32 + which * 8 + c
VC_CONVW = 64
VC_CONVB = 80
VC_BR = 84
VC_BI = 88
VC_LAM = 92
VC_SCW = 96
VC_PSC = 108
VC_NL = 112
VC_NL2 = 116
VC_EPS = 120
VC_ONE = 121
VC_T0 = 122


PHASE_MARKS = []


def build_program(layers=(0, 1)):
    del PHASE_MARKS[:]
    nc = bass.Bass("TRN2", target_bir_lowering=False)

    def din(name, shape):
        return nc.dram_tensor(name, list(shape), F32, kind="ExternalInput").ap()

    x = din("x", [SEQ, D])
    e_w_in = din("e_w_in", [D, 2560])
    e_w_out = din("e_w_out", [D, D])
    o_w_in = din("o_w_in", [D, 1536])
    o_w_out = din("o_w_out", [D, D])
    w_gate = din("w_gate", [2, D, HID])
    w_up = din("w_up", [2, D, HID])
    w_down = din("w_down", [2, HID, D])
    pool_w = din("pool_w", [P, 4, P])
    wri = din("wri", [P, 8, P])
    sgwT = din("sgwT", [P, 8, P])
    sgb = din("sgb", [P, 4, P])
    sgn = din("sgn", [P, 512])
    vecs = din("vecs", [P, P])
    invc = din("invc", [P, 4, 16])
    out = nc.dram_tensor("out", [SEQ, D], F32, kind="ExternalOutput").ap()

    with ExitStack() as st:
        S = Sched(nc, st)
        op = S.op
        H = S.sbuf("H", [P, KC, T], F32)
        BIG = S.sbuf("BIG", [P, 16384], BF16)
        XNv = BIG[:, 0:8192].rearrange("p (t c n) -> p t c n", t=NTILE, c=KC)
        BIGF = BIG[:].bitcast(F32)
        MFv = BIGF.rearrange("p (t c n) -> p t c n", t=NTILE, c=KC)

        def XNt(c, tile):
            return XNv[:, tile, c, :]

        def MFt(c, tile):
            return MFv[:, tile, c, :]
        A = S.sbuf("A", [P, HC, T], BF16)
        Y = A
        SQR = Rot([S.sbuf(f"sq{i}", [P, NT], BF16) for i in range(4)])
        RSTD = Rot([S.sbuf(f"rstd{i}", [P, NT + 8], F32) for i in range(2)])
        TMP = Rot([S.sbuf(f"tmp{i}", [P, TMPW], F32) for i in range(16)])
        TMPB = Rot([S.sbuf(f"tmpb{i}", [P, NT], BF16) for i in range(4)])
        STAT = S.sbuf("STAT", [P, 8, 64], F32)
        VEC = S.sbuf("VEC", [P, P], F32)
        WRI = S.sbuf("WRI", [P, 8, P], BF16)
        SGW = S.sbuf("SGW", [P, 8, P], BF16)
        POOLW = S.sbuf("POOLW", [P, 4, P], BF16)
        SGB = S.sbuf("SGB", [P, 4, P], F32)
        SGN = S.sbuf("SGN", [P, 512], F32)
        INVC = S.sbuf("INVC", [P, 4, 16], F32)
        IDENT = S.sbuf("IDENT", [P, P], F32)
        ONES = S.sbuf("ONES", [P, P], BF16)
        CARX = S.sbuf("CARX", [P, 4, 4], F32)
        CARP = S.sbuf("CARP", [P, 4, 4], F32)
        CARH = S.sbuf("CARH", [P, 4], F32)
        CARQ = S.sbuf("CARQ", [P, 4, 16], F32)
        PS = PsumAlloc(S)
        NSTG = 4
        xsem = [S.dma_sem(f"xsem{i}") for i in range(NSTG)]
        osem = [S.dma_sem(f"osem{i}") for i in range(NSTG)]

        def vc(col):
            return VEC[:, col:col + 1]

        ewv = e_w_in.rearrange("(kc p) n -> p kc n", p=P)
        owv = o_w_in.rearrange("(kc p) n -> p kc n", p=P)
        wouts = [e_w_out.rearrange("(kc p) n -> p kc n", p=P), o_w_out.rearrange("(kc p) n -> p kc n", p=P)]

        def k3(v, k):
            return v.rearrange("p (k n) -> p k n", k=k)

        def unit(tag, src3):
            k, n = int(src3.shape[1]), int(src3.shape[2])
            return (tag, k * n, lambda v, src3=src3, k=k: [(k3(v, k), src3)])

        plan = []
        for ps_ in range(NPASS):
            for l in layers:
                if l == 0:
                    for nm, g in (("xa", 0), ("ga", 1), ("cg", 3), ("hs", 4), ("bg", 2)):
                        plan.append(unit(("ein", nm), ewv[:, :, g * 512:(g + 1) * 512]))
                else:
                    for nm, g in (("xp", 0), ("v", 2), ("u", 1)):
                        plan.append(unit(("oin", nm), owv[:, :, g * 512:(g + 1) * 512]))
                for g in range(2):
                    plan.append(unit(("wout", g), wouts[l][:, :, g * 512:(g + 1) * 512]))
                gv = w_gate[l].rearrange("(kc p) n -> p kc n", p=P)
                uv = w_up[l].rearrange("(kc p) n -> p kc n", p=P)
                for g in range(6):
                    n0 = g * 512
                    nn = min(512, HID - n0)
                    plan.append(unit(("gate", g), gv[:, :, n0:n0 + nn]))
                    plan.append(unit(("up", g), uv[:, :, n0:n0 + nn]))
                dv = w_down[l].rearrange("(hc p) n -> p hc n", p=P)
                for g in range(4):
                    for kh in range(2):
                        plan.append(unit(("down", g, kh), dv[:, kh * 11:(kh + 1) * 11, g * 256:(g + 1) * 256]))
        WR = WRing(S, RING_COLS, plan)

        ncl = [0]

        def cload(dst, src, q="sp"):
            ncl[0] += 1
            op(q, lambda e: e.dma_start(out=dst, in_=src), writes=[dst], dma=S.dma_sem(f"csem{ncl[0]}"))

        cload(VEC[:], vecs[:])
        cload(SGB[:], sgb[:])
        cload(SGN[:], sgn[:])
        cload(INVC[:], invc[:])
        cload(WRI[:], wri[:], "pool")
        cload(SGW[:], sgwT[:], "pool")
        cload(POOLW[:], pool_w[:], "pool")
        op("dve", lambda e: e.memset(SGW[64:128, :, 0:64], 0.0), reads=[SGW[:]], writes=[SGW[:]])
        op("pool", lambda e: e.memset(IDENT[:], 0.0), writes=[IDENT[:]])
        op("pool", lambda e: e.affine_select(out=IDENT[:], in_=IDENT[:], pattern=[[-1, P]], compare_op=ALU.not_equal,
                                             fill=1.0, base=0, channel_multiplier=1),
           reads=[IDENT[:]], writes=[IDENT[:]])
        op("dve", lambda e: e.memset(ONES[:], 1.0), writes=[ONES[:]])
        op("dve", lambda e: e.memset(vc(VC_EPS), EPS), reads=[VEC[:]], writes=[VEC[:]])
        op("dve", lambda e: e.memset(vc(VC_ONE), 1.0), reads=[VEC[:]], writes=[VEC[:]])
        for tcar in (CARX, CARP, CARH, CARQ):
            op("dve", lambda e, tcar=tcar: e.memset(tcar[:], 0.0), writes=[tcar[:]])
        yv = VEC[:, VC_T0:VC_T0 + 4]
        nl = VEC[:, VC_NL:VC_NL + 4]
        nl2 = VEC[:, VC_NL2:VC_NL2 + 4]
        op("act", lambda e: e.activation(out=yv, in_=VEC[:, VC_LAM:VC_LAM + 4], func=AF.Exp, scale=-1.0),
           reads=[VEC[:]], writes=[VEC[:]])
        op("dve", lambda e: e.tensor_scalar(out=nl, in0=yv, scalar1=1.0 / 5.0, scalar2=-0.25, op0=ALU.mult, op1=ALU.add),
           reads=[VEC[:]], writes=[VEC[:]])
        for cst in (1.0 / 3.0, -0.5, 1.0):
            op("dve", lambda e: e.tensor_tensor(out=nl, in0=nl, in1=yv, op=ALU.mult), reads=[VEC[:]], writes=[VEC[:]])
            op("dve", lambda e, cst=cst: e.tensor_scalar(out=nl, in0=nl, scalar1=cst, scalar2=None, op0=ALU.add),
               reads=[VEC[:]], writes=[VEC[:]])
        op("dve", lambda e: e.tensor_tensor(out=nl, in0=nl, in1=yv, op=ALU.mult), reads=[VEC[:]], writes=[VEC[:]])
        op("dve", lambda e: e.tensor_scalar(out=nl2, in0=nl, scalar1=-2.0 * LRU_C, scalar2=None, op0=ALU.mult),
           reads=[VEC[:]], writes=[VEC[:]])
        op("dve", lambda e: e.tensor_scalar(out=nl, in0=nl, scalar1=-LRU_C, scalar2=None, op0=ALU.mult),
           reads=[VEC[:]], writes=[VEC[:]])

        def mm_group(ps_ap, lhs, rhs):
            n = len(lhs)
            for i in range(n):
                op("pe", lambda e, i=i: e.matmul(ps_ap, lhsT=lhs[i], rhs=rhs[i], start=(i == 0), stop=(i == n - 1)),
                   reads=[lhs[i], rhs[i]], writes=[ps_ap])

        def tks(tile):
            return slice(tile * NT, (tile + 1) * NT)

        def rstd_from(ps, d_inv):
            rs = RSTD.get()[:, 0:NT]
            op("act", lambda e: e.activation(out=rs[:], in_=ps[:], func=AF.Ln, scale=d_inv, bias=vc(VC_EPS)),
               reads=[ps[:], VEC[:]], writes=[rs[:]])
            op("act", lambda e: e.activation(out=rs[:], in_=rs[:], func=AF.Exp, scale=-0.5), reads=[rs[:]], writes=[rs[:]])
            return rs

        def sq_accum(pst, src, c, defer=None, sqrot=None):
            sq = (sqrot or SQR).get()
            op("act", lambda e: e.activation(out=sq[:], in_=src, func=AF.Square), reads=[src], writes=[sq[:]])

            def pe_half():
                op("pe", lambda e: e.matmul(pst[:], lhsT=ONES[:], rhs=sq[:], start=(c == 0), stop=(c == KC - 1)),
                   reads=[ONES[:], sq[:]], writes=[pst[:]])
            if defer is None:
                pe_half()
            else:
                defer.append(pe_half)
                while len(defer) > 2:
                    defer.pop(0)()

        def xn_from_H(l, which, tile, rs):
            tk = tks(tile)
            for c in range(KC):
                g = vc(VC_NORM(l, which, c))
                op("dve", lambda e, c=c, g=g: e.scalar_tensor_tensor(
                    out=XNt(c, tile), in0=H[:, c, tk], scalar=g, in1=rs[:], op0=ALU.mult, op1=ALU.mult),
                   reads=[H[:, c, tk], VEC[:], rs[:]], writes=[XNt(c, tile)])

        def pre_norm(l, which):
            for tile in range(NTILE):
                tk = tks(tile)
                pst = PS.get()
                for c in range(KC):
                    sq_accum(pst, H[:, c, tk], c)
                rs = rstd_from(pst, 1.0 / D)
                xn_from_H(l, which, tile, rs)

        PEND = []

        def prenorm_gen(l, which, tile):
            tk = tks(tile)
            hh = PS.hold(1)
            pst = PS.b[hh[0]]
            for c in range(KC):
                sq_accum(pst, H[:, c, tk], c, defer=PEND)
                yield
            while PEND:
                PEND.pop(0)()
            rs = rstd_from(pst, 1.0 / D)
            PS.release(hh)
            xn_from_H(l, which, tile, rs)

        def flush_pend():
            while PEND:
                PEND.pop(0)()

        def proj_gen(units, nq, nk, rhs_of, l, which_post, tile, pst):
            for (g, W) in units:
                for q in range(nq):
                    dc = g * nq + q
                    ps = PS.get()
                    mm_group(ps[:], [W(k, q) for k in range(nk)], [rhs_of(k, tile) for k in range(nk)])
                    flush_pend()
                    sq_accum(pst, ps[:], dc, defer=PEND)
                    gp = vc(VC_NORM(l, which_post, dc))
                    op("act", lambda e, ps=ps, dc=dc, gp=gp: e.activation(out=MFt(dc, tile), in_=ps[:], func=AF.Copy, scale=gp),
                       reads=[ps[:], VEC[:]], writes=[MFt(dc, tile)])
                    yield

        def site_gen(pst, tile, nxt, release=None, sqrot=None):
            tk = tks(tile)
            flush_pend()
            rs = rstd_from(pst, 1.0 / D)
            h2 = PS.hold(1) if nxt is not None else None
            ps2 = PS.b[h2[0]] if nxt is not None else None
            for c in range(KC):
                op("dve", lambda e, c=c: e.tensor_tensor(out=MFt(c, tile), in0=MFt(c, tile), in1=rs[:], op=ALU.mult),
                   reads=[MFt(c, tile), rs[:]], writes=[MFt(c, tile)])
                op("dve", lambda e, c=c: e.tensor_tensor(out=H[:, c, tk], in0=H[:, c, tk], in1=MFt(c, tile), op=ALU.add),
                   reads=[H[:, c, tk], MFt(c, tile)], writes=[H[:, c, tk]])
                if nxt is not None:
                    sq_accum(ps2, H[:, c, tk], c, defer=PEND, sqrot=sqrot)
                yield
            flush_pend()
            if nxt is not None:
                rs2 = rstd_from(ps2, 1.0 / D)
                PS.release(h2)
                xn_from_H(nxt[0], nxt[1], tile, rs2)
            if release is not None:
                PS.release(release)

        def run(gen):
            for _ in gen:
                pass

        def lockstep(gens, offsets=None):
            gens = list(gens)
            offs = list(offsets) if offsets is not None else [0] * len(gens)
            live = list(zip(offs, gens))
            rnd = 0
            while live:
                nxt_live = []
                for o, g_ in live:
                    if o > rnd:
                        nxt_live.append((o, g_))
                        continue
                    try:
                        next(g_)
                        nxt_live.append((o, g_))
                    except StopIteration:
                        pass
                live = nxt_live
                rnd += 1

        def pipeline(items, stageA, stageB):
            ctx = []
            for i, it in enumerate(items):
                ctx.append(stageA(it))
                if i >= 1:
                    stageB(items[i - 1], ctx[i - 1])
            if items:
                stageB(items[-1], ctx[-1])

        def ffn(l, nxt, pending, before_down=None):
            def gate_steps(Wg, g, nq, tile):
                tk = tks(tile)
                for q in range(nq):
                    hc = g * 4 + q
                    ps = PS.get()
                    mm_group(ps[:], [Wg[:, k, q * P:(q + 1) * P] for k in range(KC)], [XNt(k, tile) for k in range(KC)])
                    flush_pend()
                    op("act", lambda e, ps=ps, hc=hc: e.activation(out=A[:, hc, tk], in_=ps[:], func=AF.Silu),
                       reads=[ps[:]], writes=[A[:, hc, tk]])
                    yield

            def up_steps(Wu, g, nq, tile):
                tk = tks(tile)
                for q in range(nq):
                    hc = g * 4 + q
                    ps = PS.get()
                    mm_group(ps[:], [Wu[:, k, q * P:(q + 1) * P] for k in range(KC)], [XNt(k, tile) for k in range(KC)])
                    flush_pend()
                    op("dve", lambda e, ps=ps, hc=hc: e.tensor_tensor(out=A[:, hc, tk], in0=A[:, hc, tk], in1=ps[:], op=ALU.mult),
                       reads=[A[:, hc, tk], ps[:]], writes=[A[:, hc, tk]])
                    yield

            def chain(*gens):
                for g_ in gens:
                    for _ in g_:
                        yield

            Wg = k3(WR.next(("gate", 0)), KC)
            Wu = k3(WR.next(("up", 0), live=1), KC)
            Wg1 = k3(WR.next(("gate", 1), live=2), KC)
            lockstep([pending, chain(gate_steps(Wg, 0, 4, 0), up_steps(Wu, 0, 4, 0), gate_steps(Wg1, 1, 4, 0))])
            run(chain(gate_steps(Wg, 0, 4, 1), up_steps(Wu, 0, 4, 1), gate_steps(Wg1, 1, 4, 1)))
            Wu1 = k3(WR.next(("up", 1)), KC)
            for tile in range(NTILE):
                run(up_steps(Wu1, 1, 4, tile))
            for g in range(2, 6):
                nq = 4 if g < 5 else 2
                Wg = k3(WR.next(("gate", g)), KC)
                for tile in range(NTILE):
                    run(gate_steps(Wg, g, nq, tile))
                Wu = k3(WR.next(("up", g)), KC)
                for tile in range(NTILE):
                    run(up_steps(Wu, g, nq, tile))
            if before_down is not None:
                before_down()
            S.mark("  down")
            held = PS.hold(NTILE)
            pst = [PS.b[k] for k in held]
            rhs_of = lambda k, tile: A[:, k, tks(tile)]
            def down_unit(g, live0):
                Wa = k3(WR.next(("down", g, 0), live=live0), 11)
                Wb = k3(WR.next(("down", g, 1), live=live0 + 1), 11)
                return (g, lambda k, q, Wa=Wa, Wb=Wb: (Wa if k < 11 else Wb)[:, k % 11, q * P:(q + 1) * P])

            for g in range(2):
                u_ = down_unit(g, 0)
                for tile in range(NTILE):
                    run(proj_gen([u_], 2, HC, rhs_of, l, 3, tile, pst[tile]))
            units = [down_unit(2, 0), down_unit(3, 2)]
            run(proj_gen(units, 2, HC, rhs_of, l, 3, 0, pst[0]))
            S.mark("  postnorm")
            lockstep([site_gen(pst[0], 0, nxt), proj_gen(units, 2, HC, rhs_of, l, 3, 1, pst[1])])
            return site_gen(pst[1], 1, nxt, release=held, sqrot=Rot(SQR.items[1:4]))

        def w_out_phase(l):
            held = PS.hold(NTILE)
            pst = [PS.b[k] for k in held]
            rhs_of = lambda k, tile: Y[:, k, tks(tile)]
            def wout_unit(g, live0):
                W = k3(WR.next(("wout", g), live=live0), KC)
                return (g, lambda k, q, W=W: W[:, k, q * P:(q + 1) * P])

            units = [wout_unit(0, 0), wout_unit(1, 1)]
            run(proj_gen(units, 4, KC, rhs_of, l, 1, 0, pst[0]))
            S.mark("  postnorm")
            lockstep([site_gen(pst[0], 0, (l, 2)), proj_gen(units, 4, KC, rhs_of, l, 1, 1, pst[1])])
            return site_gen(pst[1], 1, (l, 2), release=held)

        def even_mixer(l, seq_first):
            items = [(j, tile) for j in range(4) for tile in range(NTILE)]
            cur = {}

            def lru_item(it):
                j, tile = it
                tk = tks(tile)
                if "xa" not in cur:
                    cur["xa"] = k3(WR.next(("ein", "xa")), KC)
                    cur["ga"] = k3(WR.next(("ein", "ga"), live=1), KC)
                kx, psx = PS.alloc()
                mm_group(psx[:], [cur["xa"][:, k, j * P:(j + 1) * P] for k in range(KC)], [XNt(k, tile) for k in range(KC)])
                yield
                XA = TMP.get()
                op("dve", lambda e: e.tensor_copy(out=XA[:, 0:3], in_=CARX[:, j, 0:3]), reads=[CARX[:, j, :]], writes=[XA[:, 0:3]])
                op("act", lambda e: e.activation(out=XA[:, 3:3 + NT], in_=psx[:], func=AF.Copy), reads=[psx[:]], writes=[XA[:, 3:3 + NT]])
                PS.free(kx)
                op("dve", lambda e: e.tensor_copy(out=CARX[:, j, 0:3], in_=XA[:, NT:NT + 3]), reads=[XA[:, NT:NT + 3]], writes=[CARX[:, j, :]])
                yield
                XC = TMP.get()
                op("dve", lambda e: e.tensor_scalar(out=XC[:, 0:NT], in0=XA[:, 0:NT], scalar1=vc(VC_CONVW + j), scalar2=vc(VC_CONVB + j),
                                                    op0=ALU.mult, op1=ALU.add),
                   reads=[XA[:, 0:NT], VEC[:]], writes=[XC[:, 0:NT]])
                for k in range(1, 4):
                    op("dve", lambda e, k=k: e.scalar_tensor_tensor(out=XC[:, 0:NT], in0=XA[:, k:k + NT], scalar=vc(VC_CONVW + 4 * k + j),
                                                                   in1=XC[:, 0:NT], op0=ALU.mult, op1=ALU.add),
                       reads=[XA[:, k:k + NT], XC[:, 0:NT], VEC[:]], writes=[XC[:, 0:NT]])
                yield
                XCB = TMPB.get()
                op("act", lambda e: e.activation(out=XCB[:], in_=XC[:, 0:NT], func=AF.Copy), reads=[XC[:, 0:NT]], writes=[XCB[:]])
                kr, psr = PS.alloc()
                mm_group(psr[:], [WRI[:, j, :]], [XCB[:]])
                ki, psi = PS.alloc()
                mm_group(psi[:], [WRI[:, 4 + j, :]], [XCB[:]])
                yield
                R = TMP.get()
                I = TMP.get()
                op("act", lambda e: e.activation(out=R[:, 0:NT], in_=psr[:], func=AF.Sigmoid, bias=vc(VC_BR + j)),
                   reads=[psr[:], VEC[:]], writes=[R[:, 0:NT]])
                op("act", lambda e: e.activation(out=I[:, 0:NT], in_=psi[:], func=AF.Sigmoid, bias=vc(VC_BI + j)),
                   reads=[psi[:], VEC[:]], writes=[I[:, 0:NT]])
                PS.free(kr)
                PS.free(ki)
                op("dve", lambda e: e.tensor_tensor(out=I[:, 0:NT], in0=I[:, 0:NT], in1=XC[:, 0:NT], op=ALU.mult),
                   reads=[I[:, 0:NT], XC[:, 0:NT]], writes=[I[:, 0:NT]])
                yield
                AA = XA
                A2 = XC
                op("act", lambda e: e.activation(out=AA[:, 0:NT], in_=R[:, 0:NT], func=AF.Exp, scale=vc(VC_NL + j)),
                   reads=[R[:, 0:NT], VEC[:]], writes=[AA[:, 0:NT]])
                op("act", lambda e: e.activation(out=A2[:, 0:NT], in_=R[:, 0:NT], func=AF.Exp, scale=vc(VC_NL2 + j)),
                   reads=[R[:, 0:NT], VEC[:]], writes=[A2[:, 0:NT]])
                yield
                op("act", lambda e: e.activation(out=A2[:, 0:NT], in_=A2[:, 0:NT], func=AF.Sqrt, scale=-1.0, bias=vc(VC_ONE)),
                   reads=[A2[:, 0:NT], VEC[:]], writes=[A2[:, 0:NT]])
                op("dve", lambda e: e.tensor_tensor(out=I[:, 0:NT], in0=I[:, 0:NT], in1=A2[:, 0:NT], op=ALU.mult),
                   reads=[I[:, 0:NT], A2[:, 0:NT]], writes=[I[:, 0:NT]])
                HS = R
                op("dve", lambda e: e.tensor_tensor_scan(out=HS[:, 0:NT], data0=AA[:, 0:NT], data1=I[:, 0:NT], initial=CARH[:, j:j + 1],
                                                         op0=ALU.mult, op1=ALU.add),
                   reads=[AA[:, 0:NT], I[:, 0:NT], CARH[:, j:j + 1]], writes=[HS[:, 0:NT]])
                op("dve", lambda e: e.tensor_copy(out=CARH[:, j:j + 1], in_=HS[:, NT - 1:NT]), reads=[HS[:, NT - 1:NT]], writes=[CARH[:, j:j + 1]])
                kg, psg = PS.alloc()
                mm_group(psg[:], [cur["ga"][:, k, j * P:(j + 1) * P] for k in range(KC)], [XNt(k, tile) for k in range(KC)])
                yield
                GE = AA
                op("act", lambda e: e.activation(out=GE[:, 0:NT], in_=psg[:], func=AF.Gelu_apprx_tanh), reads=[psg[:]], writes=[GE[:, 0:NT]])
                PS.free(kg)
                op("dve", lambda e: e.tensor_tensor(out=Y[:, j, tk], in0=HS[:, 0:NT], in1=GE[:, 0:NT], op=ALU.mult),
                   reads=[HS[:, 0:NT], GE[:, 0:NT]], writes=[Y[:, j, tk]])

            def sc_tile(tile):
                tk = tks(tile)
                if "cg" not in cur:
                    cur["cg"] = k3(WR.next(("ein", "cg"), live=2), KC)
                    cur["hs"] = k3(WR.next(("ein", "hs"), live=3), KC)
                    cur["bg"] = k3(WR.next(("ein", "bg"), live=4), KC)
                if tile == 0:
                    CG = RSTD.items[0]
                    PP = RSTD.items[1]
                else:
                    CG = BIGF[:, 4096:4096 + NT + 8]
                    PP = BIGF[:, 4096 + 544:4096 + 544 + NT + 8]

                def mm(nm, j):
                    k_, ps = PS.alloc()
                    mm_group(ps[:], [cur[nm][:, k, j * P:(j + 1) * P] for k in range(KC)], [XNt(k, tile) for k in range(KC)])
                    return k_, ps

                for j in range(4):
                    kc_, psc = mm("cg", j)
                    yield
                    op("act", lambda e, psc=psc: e.activation(out=CG[:, 0:NT], in_=psc[:], func=AF.Copy), reads=[psc[:]], writes=[CG[:, 0:NT]])
                    PS.free(kc_)
                    kh_, psh = mm("hs", j)
                    yield
                    op("dve", lambda e, j=j: e.tensor_copy(out=PP[:, 0:2], in_=CARP[:, j, 0:2]), reads=[CARP[:, j, :]], writes=[PP[:, 0:2]])
                    op("dve", lambda e, psh=psh: e.tensor_tensor(out=PP[:, 2:2 + NT], in0=CG[:, 0:NT], in1=psh[:], op=ALU.mult),
                       reads=[CG[:, 0:NT], psh[:]], writes=[PP[:, 2:2 + NT]])
                    op("dve", lambda e, j=j: e.tensor_copy(out=CARP[:, j, 0:2], in_=PP[:, NT:NT + 2]), reads=[PP[:, NT:NT + 2]], writes=[CARP[:, j, :]])
                    PS.free(kh_)
                    kb_, psb = mm("bg", j)
                    yield
                    CV = CG
                    op("dve", lambda e, j=j: e.tensor_scalar(out=CV[:, 0:NT], in0=PP[:, 0:NT], scalar1=vc(VC_SCW + j), scalar2=None, op0=ALU.mult),
                       reads=[PP[:, 0:NT], VEC[:]], writes=[CV[:, 0:NT]])
                    for k in range(1, 3):
                        op("dve", lambda e, k=k, j=j: e.scalar_tensor_tensor(out=CV[:, 0:NT], in0=PP[:, k:k + NT], scalar=vc(VC_SCW + 4 * k + j),
                                                                            in1=CV[:, 0:NT], op0=ALU.mult, op1=ALU.add),
                           reads=[PP[:, k:k + NT], CV[:, 0:NT], VEC[:]], writes=[CV[:, 0:NT]])
                    op("dve", lambda e, j=j, psb=psb: e.tensor_tensor(out=Y[:, 4 + j, tk], in0=CV[:, 0:NT], in1=psb[:], op=ALU.mult),
                       reads=[CV[:, 0:NT], psb[:]], writes=[Y[:, 4 + j, tk]])
                    PS.free(kb_)
                    yield

            def chain_gens(*gens):
                for g_ in gens:
                    for _ in g_:
                        yield

            nsteps = 9
            gens = [lru_item((j, 0)) for j in range(4)] + [lru_item((j, 1)) for j in range(4)]
            offs = tuple(LRU_OFFS) + tuple(o + nsteps for o in LRU_OFFS)
            lockstep(gens + [sc_tile(0), sc_tile(1)], offsets=offs + SC_OFFS)
            S.mark('  sc')
            S.mark('  wout')
            return w_out_phase(l)

        def odd_mixer(l, seq_first, pending=None):
            items = [(g, tile) for g in range(4) for tile in range(NTILE)]
            cur = {}

            TMPP = Rot(TMP.items[0:13])

            def pool_item(it):
                g, tile = it
                tk = tks(tile)
                win = WINS[g]
                if "w" not in cur:
                    cur["w"] = k3(WR.next(("oin", "xp")), KC)
                Wp = cur["w"]
                kps, ps = PS.alloc()
                mm_group(ps[:], [Wp[:, k, g * P:(g + 1) * P] for k in range(KC)], [XNt(k, tile) for k in range(KC)])
                yield
                XP = TMPP.get()
                LA = TMPP.get()
                LB = TMPP.get() if g >= 1 else None
                op("dve", lambda e: e.tensor_copy(out=XP[:, 1:16], in_=CARQ[:, g, 0:15]), reads=[CARQ[:, g, :]], writes=[XP[:, 1:16]])
                op("act", lambda e: e.activation(out=XP[:, 16:16 + NT], in_=ps[:], func=AF.Copy), reads=[ps[:]], writes=[XP[:, 16:16 + NT]])
                PS.free(kps)
                op("dve", lambda e: e.tensor_copy(out=CARQ[:, g, 0:15], in_=XP[:, NT + 1:NT + 16]), reads=[XP[:, NT + 1:NT + 16]], writes=[CARQ[:, g, :]])
                yield
                L = XP
                for lev in range(g + 1):
                    sh = 1 << lev
                    v = 2 * sh
                    NL = LA if lev % 2 == 0 else LB
                    op("dve", lambda e, L=L, NL=NL, v=v, sh=sh: e.tensor_tensor(out=NL[:, v:16 + NT], in0=L[:, v:16 + NT],
                                                                                in1=L[:, v - sh:16 + NT - sh], op=ALU.add),
                       reads=[L[:, v - sh:16 + NT]], writes=[NL[:, v:16 + NT]])
                    L = NL
                    yield
                PB = TMPB.get()
                op("dve", lambda e: e.scalar_tensor_tensor(out=PB[:], in0=L[:, 16:16 + NT], scalar=1.0 / win, in1=XP[:, 16:16 + NT],
                                                           op0=ALU.mult, op1=ALU.subtract),
                   reads=[L[:, 16:16 + NT], XP[:, 16:16 + NT]], writes=[PB[:]])
                if seq_first and tile == 0:
                    T1 = STAT[:, g, 0:16]
                    op("dve", lambda e: e.tensor_tensor(out=T1, in0=L[:, 16:32], in1=INVC[:, g, :], op=ALU.mult),
                       reads=[L[:, 16:32], INVC[:]], writes=[T1])
                    op("dve", lambda e: e.tensor_tensor(out=PB[:, 0:16], in0=T1, in1=XP[:, 16:32], op=ALU.subtract),
                       reads=[T1, XP[:, 16:32]], writes=[PB[:, 0:16]])
                kpy, psy = PS.alloc()
                mm_group(psy[:], [POOLW[:, g, :]], [PB[:]])
                yield
                op("act", lambda e: e.activation(out=Y[:, g, tk], in_=psy[:], func=AF.Copy, scale=vc(VC_PSC + g)),
                   reads=[psy[:], VEC[:]], writes=[Y[:, g, tk]])
                PS.free(kpy)

            def spatial_tile(tile, blocks=(0, 1, 2, 3), chain=None):
                if "v" not in cur:
                    cur["v"] = k3(WR.next(("oin", "v"), live=1), KC)
                    cur["u"] = k3(WR.next(("oin", "u"), live=2), KC)
                Wv, Wu = cur["v"], cur["u"]
                vn_i = tile if chain is None else chain
                for b in blocks:
                    t0 = tile * NT + b * P
                    xb = [XNt(k, tile)[:, b * P:(b + 1) * P] for k in range(KC)]
                    kv, psv = PS.alloc()
                    mm_group(psv[:], xb, [Wv[:, k, :] for k in range(KC)])
                    yield
                    si = b % 4 + 4
                    VN = SQR.items[vn_i]
                    JK = VN
                    ssq = STAT[:, si, 2 * tile:2 * tile + 1]
                    rt = STAT[:, si, 2 * tile + 1:2 * tile + 2]
                    op("act", lambda e, JK=JK, psv=psv, ssq=ssq: e.activation(out=JK[:], in_=psv[:], func=AF.Square, accum_out=ssq),
                       reads=[psv[:]], writes=[JK[:], ssq])
                    op("act", lambda e, ssq=ssq, rt=rt: e.activation(out=rt, in_=ssq, func=AF.Ln, scale=1.0 / 512.0, bias=vc(VC_EPS)),
                       reads=[ssq, VEC[:]], writes=[rt])
                    op("act", lambda e, rt=rt: e.activation(out=rt, in_=rt, func=AF.Exp, scale=-0.5), reads=[rt], writes=[rt])
                    op("dve", lambda e, VN=VN, psv=psv, rt=rt: e.scalar_tensor_tensor(out=VN[:], in0=psv[:], scalar=rt, in1=SGN[:],
                                                                                   op0=ALU.mult, op1=ALU.mult),
                       reads=[psv[:], rt, SGN[:]], writes=[VN[:]])
                    PS.free(kv)
                    yield
                    kg, psg = PS.alloc()
                    for h in range(8):
                        o = psg[(h % 2) * 64:(h % 2) * 64 + 64, (h // 2) * P:(h // 2 + 1) * P]
                        op("pe", lambda e, o=o, VN=VN, h=h: e.matmul(o, lhsT=VN[:, h * 64:(h + 1) * 64], rhs=SGW[:, h, :], start=True, stop=True),
                           reads=[VN[:], SGW[:, h, :]], writes=[o])
                    ku, psu = PS.alloc()
                    for j in range(4):
                        mm_group(psu[:, j * P:(j + 1) * P], [Wu[:, k, j * P:(j + 1) * P] for k in range(KC)], xb)
                    yield
                    T1 = TMP.items[13 + vn_i]
                    t1v = T1[:, 0:NT].rearrange("p (j t) -> p j t", j=4)
                    pgv = psg[:].rearrange("p (j t) -> p j t", j=4)
                    puv = psu[:].rearrange("p (j t) -> p j t", j=4)
                    yo = Y[:, 4:8, t0:t0 + P]
                    op("dve", lambda e, t1v=t1v, pgv=pgv: e.tensor_tensor(out=t1v, in0=pgv, in1=SGB[:], op=ALU.add),
                       reads=[psg[:], SGB[:]], writes=[T1[:, 0:NT]])
                    op("dve", lambda e, t1v=t1v, puv=puv, yo=yo: e.tensor_tensor(out=yo, in0=t1v, in1=puv, op=ALU.mult),
                       reads=[T1[:, 0:NT], psu[:]], writes=[yo])
                    PS.free(kg)
                    PS.free(ku)
                    yield

            def chain_gens(*gens):
                for g_ in gens:
                    for _ in g_:
                        yield

            gens = [pool_item((g, 0)) for g in (3, 2, 1, 0)] + [pool_item((g, 1)) for g in (3, 2, 1, 0)]
            offs = (0, 0, 1, 1, 9, 9, 10, 10)
            pre = [pending] if pending is not None else []
            sp = [spatial_tile(0, chain=0), spatial_tile(1, (0, 1), chain=1), spatial_tile(1, (2, 3), chain=2)]
            lockstep(pre + gens + sp, offsets=(0,) * len(pre) + offs + SP_OFFS)
            S.mark('  spatial')
            S.mark('  wout')
            return w_out_phase(l)

        STG_HI = [BIGF[:, (4 + i) * 1024:(5 + i) * 1024] for i in range(NSTG)]
        STG_LO = [BIGF[:, i * 1024:(i + 1) * 1024] for i in range(NSTG)]
        stg_cnt = {"x": 0, "o": 0}

        xstage = {}
        xsem16 = [S.dma_sem(f"xs{i}") for i in range(16)]

        def issue_loads(ps_, blks):
            t0 = ps_ * T
            for blk in blks:
                for half in range(2):
                    i = (2 * blk + half) % 16
                    xt = TMP.items[i]
                    xstage[(ps_, blk, half)] = xt
                    op("sp", lambda e, xt=xt, blk=blk, half=half: e.dma_start(
                        out=xt[:, 0:512], in_=x[t0 + blk * P:t0 + (blk + 1) * P, half * 512:(half + 1) * 512]),
                       writes=[xt[:, 0:512]], dma=xsem16[i])

        def load_gen(ps_, blks):
            for blk in blks:
                if (ps_, blk, 0) not in xstage:
                    issue_loads(ps_, [blk])
                for half in range(2):
                    xt = xstage.pop((ps_, blk, half))
                    ps = PS.get()
                    for q in range(4):
                        op("pe", lambda e, ps=ps, q=q, xt=xt: e.transpose(out=ps[:, q * P:(q + 1) * P], in_=xt[:, q * P:(q + 1) * P], identity=IDENT[:]),
                           reads=[xt[:, q * P:(q + 1) * P], IDENT[:]], writes=[ps[:, q * P:(q + 1) * P]])
                    dst = H[:, half * 4:half * 4 + 4, blk * P:(blk + 1) * P]
                    src = ps[:].rearrange("p (q t) -> p q t", q=4)
                    if half == 0:
                        op("act", lambda e, dst=dst, src=src: e.activation(out=dst, in_=src, func=AF.Copy), reads=[ps[:]], writes=[dst])
                    else:
                        op("dve", lambda e, dst=dst, src=src: e.tensor_copy(out=dst, in_=src), reads=[ps[:]], writes=[dst])
                yield

        out_toks = []

        def store_gen(ps_, blks, stg):
            t0 = ps_ * T
            for blk in blks:
                i = stg_cnt["o"] % NSTG
                stg_cnt["o"] += 1
                ot = stg[i]
                for half in range(2):
                    ps = PS.get()
                    for q in range(4):
                        c = half * 4 + q
                        src = H[:, c, blk * P:(blk + 1) * P]
                        op("pe", lambda e, ps=ps, q=q, src=src: e.transpose(out=ps[:, q * P:(q + 1) * P], in_=src, identity=IDENT[:]),
                           reads=[src, IDENT[:]], writes=[ps[:, q * P:(q + 1) * P]])
                    dst = ot[:, half * 512:(half + 1) * 512]
                    if half == 0:
                        op("act", lambda e, dst=dst, ps=ps: e.activation(out=dst, in_=ps[:], func=AF.Copy), reads=[ps[:]], writes=[dst])
                    else:
                        op("dve", lambda e, dst=dst, ps=ps: e.tensor_copy(out=dst, in_=ps[:]), reads=[ps[:]], writes=[dst])
                tok = op("pool", lambda e, ot=ot, blk=blk: e.dma_start(out=out[t0 + blk * P:t0 + (blk + 1) * P, :], in_=ot),
                         reads=[ot], dma=osem[i])
                out_toks.append(tok)
                yield

        def mark(name):
            PHASE_MARKS.append((name, len(S.ops["pe"])))

        S.mark = mark
        NB = T // P // NTILE
        carry_site = None
        for ps_ in range(NPASS):
            mark(f"p{ps_} load")
            if ps_ == 0:
                issue_loads(0, range(0, 2 * NB))
                S.wait_all("pool", [(xsem16[i], S.semcnt[xsem16[i]]) for i in range(8)])
            run(load_gen(ps_, range(0, NB)))
            lockstep([prenorm_gen(layers[0], 0, 0), load_gen(ps_, range(NB, 2 * NB))])
            run(prenorm_gen(layers[0], 0, 1))
            for li, l in enumerate(layers):
                mark(f"p{ps_} L{l} mixer")
                if l == 0:
                    pending = even_mixer(l, ps_ == 0)
                else:
                    pending = odd_mixer(l, ps_ == 0, carry_site)
                mark(f"p{ps_} L{l} ffn")
                nxt = (layers[li + 1], 0) if li + 1 < len(layers) else None
                pre = None
                if nxt is None and ps_ + 1 < NPASS:
                    pre = lambda ps_=ps_: issue_loads(ps_ + 1, range(0, 2 * NB))
                pending = ffn(l, nxt, pending, pre)
                carry_site = None
                if nxt is not None and nxt[0] == 1:
                    carry_site = pending
                else:
                    run(pending)
            mark(f"p{ps_} store")
            run(store_gen(ps_, range(0, 2 * NB), STG_HI))
        last = {}
        for k, v in out_toks:
            last[k] = max(last.get(k, 0), v)
        S.wait_all("sp", list(last.items()))
        S.emit()
    return nc


def _host_layout(inp):
    f = np.float32
    vec = np.zeros((P, P), f)

    def put(col, v):
        v = np.asarray(v, f).reshape(-1, P)
        vec[:, col:col + v.shape[0]] = v.T

    for l in range(2):
        put(VC_NORM(l, 0, 0), inp["norm_mix_pre"][l])
        put(VC_NORM(l, 1, 0), inp["norm_mix_post"][l])
        put(VC_NORM(l, 2, 0), inp["norm_ffn_pre"][l])
        put(VC_NORM(l, 3, 0), inp["norm_ffn_post"][l])
    for k in range(4):
        put(VC_CONVW + 4 * k, inp["e_conv_w"][0, k])
    put(VC_CONVB, inp["e_conv_b"][0])
    put(VC_BR, inp["e_b_r"][0].reshape(-1))
    put(VC_BI, inp["e_b_i"][0].reshape(-1))
    put(VC_LAM, inp["e_lam"][0])
    for k in range(3):
        put(VC_SCW + 4 * k, inp["e_sc_conv_w"][0, k])
    put(VC_PSC, inp["o_pool_scale"][0])
    wri = np.zeros((P, 8, P), f)
    for which, w in enumerate((inp["e_w_r"][0], inp["e_w_i"][0])):
        for h in range(8):
            j, o = h // 2, (h % 2) * 64
            wri[o:o + 64, which * 4 + j, o:o + 64] = w[h]
    sgwT = np.ascontiguousarray(np.transpose(np.asarray(inp["o_sg_w"][0], f), (2, 0, 1)))
    sb = np.asarray(inp["o_sg_b"][0], f)
    sgb = np.ascontiguousarray(np.repeat(sb.reshape(4, 2, 1, P), 64, axis=2).reshape(4, P, P).transpose(1, 0, 2))
    sgn = np.ascontiguousarray(np.broadcast_to(np.asarray(inp["o_sg_norm"][0], f)[None, :], (P, 512)))
    pw = np.ascontiguousarray(np.transpose(np.asarray(inp["o_pool_w"][0], f), (1, 0, 2)))
    invc = np.zeros((P, 4, 16), f)
    for g, win in enumerate(WINS):
        invc[:, g, :] = 1.0 / np.minimum(np.arange(1, 17), win).astype(f)
    return {
        "e_w_in": np.ascontiguousarray(inp["e_w_in"][0], f), "e_w_out": np.ascontiguousarray(inp["e_w_out"][0], f),
        "o_w_in": np.ascontiguousarray(inp["o_w_in"][0], f), "o_w_out": np.ascontiguousarray(inp["o_w_out"][0], f),
        "w_gate": np.ascontiguousarray(inp["w_gate"], f), "w_up": np.ascontiguousarray(inp["w_up"], f),
        "w_down": np.ascontiguousarray(inp["w_down"], f),
        "pool_w": pw, "wri": wri, "sgwT": sgwT, "sgb": sgb, "sgn": sgn, "vecs": vec, "invc": invc,
    }


def kernel(**inputs):
    inp = {k: np.asarray(v) for k, v in inputs.items()}
    shared = _host_layout(inp)
    x = np.ascontiguousarray(inp["x"], np.float32)
    nc = build_program((0, 1))
    in_maps = [dict(shared, x=x[b]) for b in range(8)]
    res = run_bass_kernel_spmd(nc, in_maps, core_ids=list(range(8)))
    return np.stack([np.asarray(r["out"], np.float32) for r in res.results], axis=0)
```
